# Optimizing a Trainium2 kernel written in Bass

```python
import jax, jax.numpy as jnp
from jax import lax
import numpy as np

D_MODEL = 1024
BATCH = 8
SEQ = 4096
DEPTH = 2

ATT_HEADS = 8
ATT_HEAD_DIM = 64
D_ATT = ATT_HEADS * ATT_HEAD_DIM
SSD_HEADS = 8
SSD_HEAD_DIM = 64
D_SSD = SSD_HEADS * SSD_HEAD_DIM
SSD_GROUPS = 2
SSD_STATE = 128
SSD_CONV = 4
SSD_CHUNK = 128
D_MIX = D_ATT + D_SSD
D_CONV_CH = D_SSD + 2 * SSD_GROUPS * SSD_STATE
D_IN = 3 * D_ATT + ATT_HEADS + D_SSD + D_CONV_CH + SSD_HEADS
Q_BLOCK = 128
D_FF = 2816
FFN_CONV = 3
N_MOD = 6
EPS = 1e-6

kernel_name = "fox_ssd_parallel_hybrid_block"


def _rmsnorm(x, g):
    xf = x.astype(jnp.float32)
    y = xf * lax.rsqrt(jnp.mean(xf * xf, axis=-1, keepdims=True) + EPS)
    return (y * g.astype(jnp.float32)).astype(x.dtype)


def _causal_dwconv(u, w, b):
    k, ch = w.shape
    out = lax.conv_general_dilated(
        u, w[:, None, :].astype(u.dtype), window_strides=(1,), padding=[(k - 1, 0)],
        dimension_numbers=('NWC', 'WIO', 'NWC'), feature_group_count=ch)
    return out + b.astype(u.dtype)


def _forgetting_attention(q, k, v, log_f):
    bsz, s, h, d = q.shape
    nb = s // Q_BLOCK
    f_cum = jnp.cumsum(log_f, axis=1).transpose(0, 2, 1)
    q_blocks = q.reshape(bsz, nb, Q_BLOCK, h, d).transpose(1, 0, 2, 3, 4)
    fq_blocks = f_cum.reshape(bsz, h, nb, Q_BLOCK).transpose(2, 0, 1, 3)
    k_pos = jnp.arange(s)
    scale = d ** -0.5

    def one_block(args):
        i, q_i, fq_i = args
        logits = jnp.einsum('bqhd,bkhd->bhqk', q_i, k).astype(jnp.float32) * scale
        logits = logits + fq_i[..., :, None] - f_cum[:, :, None, :]
        q_pos = i * Q_BLOCK + jnp.arange(Q_BLOCK)
        mask = k_pos[None, :] <= q_pos[:, None]
        logits = jnp.where(mask, logits, -jnp.inf)
        p = jax.nn.softmax(logits, axis=-1)
        return jnp.einsum('bhqk,bkhd->bqhd', p.astype(v.dtype), v)

    out = lax.map(one_block, (jnp.arange(nb), q_blocks, fq_blocks))
    return out.transpose(1, 0, 2, 3, 4).reshape(bsz, s, h * d)


def _ssd_chunked(xs, dt, a_neg, b_in, c_in, d_skip):
    bsz, s, h, p = xs.shape
    g, n = b_in.shape[-2:]
    r = h // g
    nc = s // SSD_CHUNK
    L = SSD_CHUNK
    dtype = xs.dtype
    xdt = (xs * dt[..., None].astype(dtype)).reshape(bsz, nc, L, g, r, p)
    a = (dt * a_neg).reshape(bsz, nc, L, g, r).transpose(0, 3, 4, 1, 2)
    a_cs = jnp.cumsum(a, axis=-1)
    bc = b_in.reshape(bsz, nc, L, g, n)
    cc = c_in.reshape(bsz, nc, L, g, n)
    causal = jnp.tril(jnp.ones((L, L), dtype=bool))
    seg = a_cs[..., :, None] - a_cs[..., None, :]
    decay_in = jnp.exp(jnp.where(causal, seg, -jnp.inf)).astype(dtype)
    cb = jnp.einsum('bclgn,bcsgn->bgcls', cc, bc)
    y_diag = jnp.einsum('bgrcls,bcsgrp->bclgrp', cb[:, :, None] * decay_in, xdt)
    decay_to_end = jnp.exp(a_cs[..., -1:] - a_cs).astype(dtype).transpose(0, 3, 4, 1, 2)
    states = jnp.einsum('bclgn,bclgrp->bcgrpn', bc, xdt * decay_to_end[..., None])
    chunk_decay = jnp.exp(a_cs[..., -1]).astype(dtype).transpose(3, 0, 1, 2)

    def step(h_prev, inp):
        st, dec = inp
        return h_prev * dec[..., None, None] + st, h_prev

    h0 = jnp.zeros((bsz, g, r, p, n), dtype)
    _, prev = lax.scan(step, h0, (states.transpose(1, 0, 2, 3, 4, 5), chunk_decay))
    prev = prev.transpose(1, 0, 2, 3, 4, 5)
    decay_from_start = jnp.exp(a_cs).astype(dtype).transpose(0, 3, 4, 1, 2)
    y_off = jnp.einsum('bclgn,bcgrpn->bclgrp', cc, prev) * decay_from_start[..., None]
    return (y_diag + y_off).reshape(bsz, s, h, p) + xs * d_skip[:, None].astype(dtype)


def _layer(x, c_act, mod_w, mod_b, norm_mix_g, norm_ffn_g, w_in, fox_forget_b, attn_norm_g,
           ssd_conv_w, ssd_conv_b, ssd_dt_bias, ssd_a_log, ssd_d, ssd_norm_g, w_out,
           ffn_w_up, ffn_conv_w, ffn_conv_b, ffn_w_down):
    bsz, s, _ = x.shape
    mod = (c_act @ mod_w + mod_b)[:, None, :]
    shift_m, scale_m, gate_m, shift_f, scale_f, gate_f = jnp.split(mod, N_MOD, axis=-1)

    h = _rmsnorm(x, norm_mix_g) * (1.0 + scale_m) + shift_m
    proj = h @ w_in
    sizes = [D_ATT, D_ATT, D_ATT, ATT_HEADS, D_SSD, D_CONV_CH, SSD_HEADS]
    q, k, v, f_logit, z, xbc, dt_raw = jnp.split(proj, list(np.cumsum(sizes)[:-1]), axis=-1)

    log_f = jax.nn.log_sigmoid(f_logit.astype(jnp.float32) + fox_forget_b.astype(jnp.float32))
    shp = (bsz, s, ATT_HEADS, ATT_HEAD_DIM)
    y_att = _forgetting_attention(q.reshape(shp), k.reshape(shp), v.reshape(shp), log_f)
    y_att = _rmsnorm(y_att, attn_norm_g)

    xbc = jax.nn.silu(_causal_dwconv(xbc, ssd_conv_w, ssd_conv_b))
    xs, b_in, c_in = jnp.split(xbc, [D_SSD, D_SSD + SSD_GROUPS * SSD_STATE], axis=-1)
    dt = jax.nn.softplus(dt_raw.astype(jnp.float32) + ssd_dt_bias.astype(jnp.float32))
    a_neg = -jnp.exp(ssd_a_log.astype(jnp.float32))
    y_ssd = _ssd_chunked(xs.reshape(bsz, s, SSD_HEADS, SSD_HEAD_DIM), dt, a_neg,
                         b_in.reshape(bsz, s, SSD_GROUPS, SSD_STATE),
                         c_in.reshape(bsz, s, SSD_GROUPS, SSD_STATE), ssd_d)
    y_ssd = y_ssd.reshape(bsz, s, D_SSD) * jax.nn.silu(z)
    y_ssd = _rmsnorm(y_ssd.reshape(bsz, s, SSD_GROUPS, D_SSD // SSD_GROUPS),
                     ssd_norm_g.reshape(SSD_GROUPS, D_SSD // SSD_GROUPS)).reshape(bsz, s, D_SSD)

    y = jnp.concatenate([y_att, y_ssd], axis=-1) @ w_out
    x = x + gate_m * y

    h = _rmsnorm(x, norm_ffn_g) * (1.0 + scale_f) + shift_f
    u = _causal_dwconv(h @ ffn_w_up, ffn_conv_w, ffn_conv_b)
    u_gate, u_val = jnp.split(u, 2, axis=-1)
    y = (jax.nn.silu(u_gate) * u_val) @ ffn_w_down
    return x + gate_f * y


def setup_inputs(seed: int = 0) -> dict:
    key = jax.random.key(seed)
    ks = jax.random.split(key, 24)
    nrm = jax.random.normal
    f32 = jnp.float32
    dt0 = jnp.exp(jax.random.uniform(ks[11], (DEPTH, SSD_HEADS), f32, np.log(1e-3), np.log(1e-1)))
    return {
        "x": nrm(ks[0], (BATCH, SEQ, D_MODEL), f32),
        "c": nrm(ks[1], (BATCH, D_MODEL), f32),
        "mod_w": nrm(ks[2], (DEPTH, D_MODEL, N_MOD * D_MODEL), f32) * (0.5 * D_MODEL ** -0.5),
        "mod_b": 0.01 * nrm(ks[3], (DEPTH, N_MOD * D_MODEL), f32),
        "norm_mix_g": 1.0 + 0.02 * nrm(ks[4], (DEPTH, D_MODEL), f32),
        "norm_ffn_g": 1.0 + 0.02 * nrm(ks[5], (DEPTH, D_MODEL), f32),
        "w_in": nrm(ks[6], (DEPTH, D_MODEL, D_IN), f32) * D_MODEL ** -0.5,
        "fox_forget_b": 3.0 + 0.5 * nrm(ks[7], (DEPTH, ATT_HEADS), f32),
        "attn_norm_g": 1.0 + 0.02 * nrm(ks[8], (DEPTH, D_ATT), f32),
        "ssd_conv_w": nrm(ks[9], (DEPTH, SSD_CONV, D_CONV_CH), f32) * SSD_CONV ** -0.5,
        "ssd_conv_b": 0.01 * nrm(ks[10], (DEPTH, D_CONV_CH), f32),
        "ssd_dt_bias": dt0 + jnp.log(-jnp.expm1(-dt0)),
        "ssd_a_log": jnp.log(jax.random.uniform(ks[12], (DEPTH, SSD_HEADS), f32, 1.0, 16.0)),
        "ssd_d": 1.0 + 0.1 * nrm(ks[13], (DEPTH, SSD_HEADS), f32),
        "ssd_norm_g": 1.0 + 0.02 * nrm(ks[14], (DEPTH, D_SSD), f32),
        "w_out": nrm(ks[15], (DEPTH, D_MIX, D_MODEL), f32) * D_MIX ** -0.5,
        "ffn_w_up": nrm(ks[16], (DEPTH, D_MODEL, 2 * D_FF), f32) * D_MODEL ** -0.5,
        "ffn_conv_w": nrm(ks[17], (DEPTH, FFN_CONV, 2 * D_FF), f32) * FFN_CONV ** -0.5,
        "ffn_conv_b": 0.01 * nrm(ks[18], (DEPTH, 2 * D_FF), f32),
        "ffn_w_down": nrm(ks[19], (DEPTH, D_FF, D_MODEL), f32) * D_FF ** -0.5,
        "final_g": 1.0 + 0.02 * nrm(ks[20], (D_MODEL,), f32),
    }


def reference(x, c, mod_w, mod_b, norm_mix_g, norm_ffn_g, w_in, fox_forget_b, attn_norm_g,
              ssd_conv_w, ssd_conv_b, ssd_dt_bias, ssd_a_log, ssd_d, ssd_norm_g, w_out,
              ffn_w_up, ffn_conv_w, ffn_conv_b, ffn_w_down, final_g):
    c_act = jax.nn.silu(c)
    for l in range(DEPTH):
        x = _layer(x, c_act, mod_w[l], mod_b[l], norm_mix_g[l], norm_ffn_g[l], w_in[l],
                   fox_forget_b[l], attn_norm_g[l], ssd_conv_w[l], ssd_conv_b[l],
                   ssd_dt_bias[l], ssd_a_log[l], ssd_d[l], ssd_norm_g[l], w_out[l],
                   ffn_w_up[l], ffn_conv_w[l], ffn_conv_b[l], ffn_w_down[l])
    return _rmsnorm(x, final_g)
```

```python
import contextlib
import os
import numpy as np
import concourse.bass as bass
import concourse.mybir as mybir
from concourse.alu_op_type import AluOpType as ALU
from concourse.bass_utils import run_bass_kernel_spmd

F32 = mybir.dt.float32
BF16 = mybir.dt.bfloat16
AF = mybir.ActivationFunctionType

ENGS = ("tensor", "vector", "scalar", "gpsimd", "sync")

S = 4096
D = 1024
L = 2
T = 512
NT = S // T
DFF = 2816
EPS = 1e-6
PIECE = 2048
NEAR = int(os.environ.get("KNEAR", "3"))
BIGTH = int(os.environ.get("KBIGTH", "128"))
BIGOK = bool(int(os.environ.get("KBIGOK", "1")))

WB = {}
_off = 0
def _wb(name, n):
    global _off
    WB[name] = (_off, n)
    _off += n
for _c in range(10):
    _wb(("INF", _c), 8 * 256)
for _c in range(2):
    _wb(("INV", _c), 8 * 256)
_wb(("INS",), 8 * 16)
for _c in range(4):
    _wb(("OUT", _c), 8 * 256)
for _b in range(22):
    _wb(("UP", _b), 8 * 256)
for _m in range(16):
    _wb(("DOWN", _m), 11 * 128)
WL = _off
WTOT = L * WL
NPIECE = (WTOT + PIECE - 1) // PIECE
WPAD = NPIECE * PIECE

VC = {}
_v = 0
for _n, _k in (("modb", 48), ("nmg", 8), ("nfg", 8), ("ang", 4), ("sng", 4), ("cw", 32), ("cb", 8),
               ("fcw", 132), ("fcb", 44), ("fin", 8)):
    VC[_n] = _v
    _v += _k
NV = _v
CC = {"ident": 0, "tri": 128, "negm": 256, "sgn": 384, "ones": 400}
NCONST = 528


class Op:
    __slots__ = ("eng", "fn", "id", "is_dma", "deps", "needs_inc", "tok", "dsem", "dtarget", "dprev", "sz")


class Prog:
    def __init__(self, nc, n_dma_sems=40):
        self.nc = nc
        self.ops = []
        self.last_w = {}
        self.readers = {}
        self.n_dma_sems = n_dma_sems

    def add(self, eng, fn, reads=(), writes=(), dma=False, sz=0):
        if eng == "MARK":
            return None
        bg = getattr(self, "bg", None)
        if bg and not getattr(self, "_in_bg", False):
            for k in reads:
                if isinstance(k, tuple) and k[0] == "wbf":
                    while bg and k not in self.last_w:
                        self._in_bg = True
                        if getattr(self, "bg_sp", None):
                            self.bg_sp.pop(0)
                        bg.pop(0)()
                        self._in_bg = False
        if bg and not getattr(self, "_in_bg", False):
            self._fg = getattr(self, "_fg", 0) + 1
            if self._fg % self.bg_every == 0:
                self._in_bg = True
                if getattr(self, "bg_sp", None):
                    self.bg_sp.pop(0)
                bg.pop(0)()
                self._in_bg = False
        op = Op()
        op.eng = eng
        op.fn = fn
        op.id = len(self.ops)
        op.is_dma = dma
        op.sz = sz
        op.needs_inc = False
        op.tok = 0
        deps = set()
        for k in reads:
            w = self.last_w.get(k)
            if w is not None:
                deps.add(w)
            if isinstance(k, tuple) and k[0] == "ps":
                for r in self.readers.get(k, ()):
                    if self.ops[r].eng != eng:
                        deps.add(r)
        for k in writes:
            w = self.last_w.get(k)
            if w is not None:
                deps.add(w)
            for r in self.readers.get(k, ()):
                deps.add(r)
        op.deps = deps
        for k in reads:
            self.readers.setdefault(k, []).append(op.id)
        for k in writes:
            self.last_w[k] = op.id
            self.readers[k] = []
        self.ops.append(op)
        return op.id

    def emit(self):
        nc = self.nc
        ops = self.ops
        for op in ops:
            for d in op.deps:
                dop = ops[d]
                if dop.is_dma:
                    continue
                if dop.eng != op.eng or op.is_dma or op.eng != "tensor":
                    dop.needs_inc = True
        cnt = {e: 0 for e in ENGS}
        dma_cnt = [0] * self.n_dma_sems
        ndma = 0
        for op in ops:
            if op.is_dma:
                k = ndma % self.n_dma_sems
                ndma += 1
                op.dsem = k
                op.dprev = dma_cnt[k]
                dma_cnt[k] += 16
                op.dtarget = dma_cnt[k]
            elif op.needs_inc:
                cnt[op.eng] += 1
                op.tok = cnt[op.eng]
        with contextlib.ExitStack() as st:
            esem = {e: st.enter_context(nc.semaphore("s_" + e)) for e in ENGS}
            dsem = [st.enter_context(nc.semaphore("d_%d" % i)) for i in range(self.n_dma_sems)]
            block = st.enter_context(nc.Block())
            per_eng = {e: [o for o in ops if o.eng == e] for e in ENGS}
            eidx = {}
            for e in ENGS:
                for n_, o in enumerate(per_eng[e]):
                    eidx[o.id] = n_
            final_dma = {}
            for o in ops:
                if o.is_dma:
                    final_dma[o.dsem] = o.dtarget
            n_dma_sems = self.n_dma_sems

            def make(ename):
                def body(eng):
                    waited_e = {e: 0 for e in ENGS}
                    waited_d = [0] * n_dma_sems
                    for op in per_eng[ename]:
                        need_e = {}
                        need_d = {}
                        for d in op.deps:
                            dop = ops[d]
                            if dop.is_dma:
                                if dop.dtarget > waited_d[dop.dsem]:
                                    need_d[dop.dsem] = max(need_d.get(dop.dsem, 0), dop.dtarget)
                            else:
                                if dop.eng == ename and not op.is_dma and (ename == "tensor" or eidx[op.id] - eidx[dop.id] > NEAR
                                                                           or (BIGOK and min(op.sz, dop.sz) >= BIGTH)):
                                    continue
                                if dop.tok > waited_e[dop.eng]:
                                    need_e[dop.eng] = max(need_e.get(dop.eng, 0), dop.tok)
                        if op.is_dma and op.dprev > waited_d[op.dsem]:
                            need_d[op.dsem] = max(need_d.get(op.dsem, 0), op.dprev)
                        for e2, v in need_e.items():
                            eng.wait_ge(esem[e2], v)
                            waited_e[e2] = v
                        for k, v in need_d.items():
                            eng.wait_ge(dsem[k], v)
                            waited_d[k] = v
                        ins = op.fn(eng)
                        if op.is_dma:
                            ins.then_inc(dsem[op.dsem], 16)
                        elif op.needs_inc:
                            ins.then_inc(esem[ename], 1)
                    if ename == "sync":
                        for k, v in final_dma.items():
                            if v > waited_d[k]:
                                eng.wait_ge(dsem[k], v)
                return body

            for e in ENGS:
                if per_eng[e] or e == "sync":
                    getattr(block, e)(make(e))


def build_nc(n_layers=L, n_tiles=NT, dbg=None, merge=True):
    nc = bass.Bass("TRN2", target_bir_lowering=False)
    x_in = nc.dram_tensor("x", [S, D], F32, kind="ExternalInput").ap()
    c_in = nc.dram_tensor("c", [128, 8], F32, kind="ExternalInput").ap()
    w_in = nc.dram_tensor("wall", [128, WPAD], F32, kind="ExternalInput").ap()
    modw_in = nc.dram_tensor("modw", [L * 12, 128, 4096], F32, kind="ExternalInput").ap()
    vec_in = nc.dram_tensor("vec", [L, 128, NV], F32, kind="ExternalInput").ap()
    bv_in = nc.dram_tensor("bv", [L, 40], F32, kind="ExternalInput").ap()
    const_in = nc.dram_tensor("consts", [128, NCONST], F32, kind="ExternalInput").ap()
    grow_in = nc.dram_tensor("grow", [L, 512], F32, kind="ExternalInput").ap()
    out = nc.dram_tensor("out", [S, D], F32, kind="ExternalOutput").ap()
    wb_d = nc.dram_tensor("wbf", [128, WPAD], BF16, kind="Internal").ap()
    x0T = nc.dram_tensor("x0T", [8, 128, S], F32, kind="Internal").ap()
    x1T = nc.dram_tensor("x1T", [8, 128, S], F32, kind="Internal").ap()
    dbg_out = None
    if dbg:
        dbg_out = {k: nc.dram_tensor("dbg_" + k, shp, F32, kind="ExternalOutput").ap() for k, shp in dbg.items()}

    P = Prog(nc)
    with contextlib.ExitStack() as st:
        def sb(name, shape, dt):
            return st.enter_context(nc.sbuf_tensor(name, shape, dt))

        KT = sb("KT", [128, 4, S], BF16)
        V = sb("V", [128, 32, 8, 65], BF16)
        gbc = sb("gbc", [128, 512], F32)
        rsm = sb("rsm", [128, 8], F32)
        NRM = int(os.environ.get("KNRM", "2"))
        ringM = [sb("wrM%d" % i, [128, 2048], BF16) for i in range(NRM)]
        ringF = [sb("wrF%d" % i, [128, 2048], BF16) for i in range(int(os.environ.get("KNRF", "2")))]
        xT = sb("xT", [128, 8, 512], F32)
        Hm = sb("Hm", [128, 8, 512], BF16)
        Hf = sb("Hf", [128, 8, 512], BF16)
        sq = sb("sq", [128, 2, 512], BF16)
        rbc = sb("rbc", [128, 512], F32)
        recip = sb("recip", [128, 512], F32)
        rbcY = recip
        ytok = rbc
        qT = sb("qT", [128, 4, 512], BF16)
        zs = sb("zs", [128, 4, 512], BF16)
        U16 = sb("U16", [128, 8, 516], F32)
        xsTb = sb("xsTb", [128, 4, 512], BF16)
        BTb = sb("BTb", [128, 2, 512], BF16)
        CTb = sb("CTb", [128, 2, 512], BF16)
        xs_tok = sb("xs_tok", [128, 512], BF16)
        xdt = sb("xdt", [128, 512], BF16)
        xdte = sb("xdte", [128, 512], BF16)
        B_tok = sb("B_tok", [128, 256], BF16)
        PT = sb("PT", [128, 2, 2, 512], BF16)
        gT = sb("gT", [128, 22, 512], BF16)
        ubuf = sb("ubuf", [128, 2, 514], BF16)
        tmp = sb("tmp", [128, 2, 512], F32)
        MT = sb("MT", [128, 8, 128], BF16)
        consts = sb("consts_sb", [128, NCONST], F32)
        identb = sb("identb", [128, 128], BF16)
        trib = sb("trib", [128, 128], BF16)
        onesb = sb("onesb", [128, 128], BF16)
        vec = sb("vec_sb", [128, L, NV], F32)
        bvb = sb("bvb", [128, L, 40], F32)
        modv = sb("modv", [128, L, 48], F32)
        gs = sb("gs", [128, L, 16], F32)
        hgf = sb("hgf", [128, L, 8], F32)
        hcw = sb("hcw", [128, 40], F32)
        cact = sb("cact", [128, 8], F32)
        F_all = sb("F_all", [128, 32, 8], F32)
        biasT = sb("biasT", [128, 2, 32, 8], F32)
        fref = sb("fref", [128, 2, 8], F32)
        carryF = sb("carryF", [128, 8], F32)
        sm16 = sb("sm16", [128, 4, 16], F32)
        a16 = sb("a16", [128, 4, 16], F32)
        acs = sb("acs", [128, 4, 16], F32)
        tot16 = sb("tot16", [128, 4, 16], F32)
        small = sb("small", [128, 8, 8], F32)
        aneg = sb("aneg", [128, 8], F32)
        Dbc = sb("Dbc", [128, 8], F32)
        Sst = sb("Sst", [128, 2, 256], F32)
        Sbf = sb("Sbf", [128, 2, 256], BF16)
        uhalo = sb("uhalo", [128, 44, 2], F32)
        wS = sb("wS", [128, 8, 16], BF16)
        pb = [st.enter_context(nc.psum_tensor("pb%d" % i, [128, 512], F32)) for i in range(8)]
        if os.environ.get("KDEBUG"):
            print("SBUF bytes remaining per partition:", nc.sbuf_bytes_remaining)
        stg = [KT[:, i, 2048:4096].bitcast(F32) for i in range(4)]
        stgk = [[("KT", i, jt) for jt in range(4, 8)] for i in range(4)]
        stgb = [V[:, 16 + 2 * i:18 + 2 * i, :, :].rearrange("p a h d -> p (a h d)")[:, 0:1024] for i in range(4)]
        stgbk = [[("V", 16 + 2 * i), ("V", 17 + 2 * i)] for i in range(4)]

        ident = consts[:, CC["ident"]:CC["ident"] + 128]
        tri = consts[:, CC["tri"]:CC["tri"] + 128]
        negm = consts[:, CC["negm"]:CC["negm"] + 128]
        sgn = consts[:, CC["sgn"]:CC["sgn"] + 16]
        onesf = consts[:, CC["ones"]:CC["ones"] + 128]

        def dma(out_ap, in_ap, reads, writes, q="sync"):
            if os.environ.get("KQ"):
                q = os.environ["KQ"]
            P.add(q, lambda e: e.dma_start(out=out_ap, in_=in_ap), reads=reads, writes=writes, dma=True)

        def mm(o, lhsT, rhs, start, stop, reads, writes):
            P.add("tensor", lambda e: e.matmul(o, lhsT=lhsT, rhs=rhs, start=start, stop=stop), reads=reads, writes=writes)

        def tr(o, in_, idn, reads, writes):
            P.add("tensor", lambda e: e.transpose(o, in_, idn), reads=reads, writes=writes)

        def _sz(o):
            try:
                return int(o.free_size())
            except Exception:
                return 0

        def act(o, in_, func, reads, writes, bias=None, scale=None):
            kw = {}
            if bias is not None:
                kw["bias"] = bias
            if scale is not None:
                kw["scale"] = scale
            P.add("scalar", lambda e: e.activation(out=o, in_=in_, func=func, **kw), reads=reads, writes=writes, sz=_sz(o))

        def tt(o, a, b, op, reads, writes):
            P.add("vector", lambda e: e.tensor_tensor(out=o, in0=a, in1=b, op=op), reads=reads, writes=writes, sz=_sz(o))

        def ptt(o, a, b, op, reads, writes):
            P.add("gpsimd", lambda e: e.tensor_tensor(out=o, in0=a, in1=b, op=op), reads=reads, writes=writes, sz=_sz(o))

        def ts(o, a, s1, s2, op0, op1, reads, writes):
            if op1 is None:
                P.add("vector", lambda e: e.tensor_scalar(out=o, in0=a, scalar1=s1, scalar2=None, op0=op0), reads=reads, writes=writes, sz=_sz(o))
            else:
                P.add("vector", lambda e: e.tensor_scalar(out=o, in0=a, scalar1=s1, scalar2=s2, op0=op0, op1=op1), reads=reads, writes=writes, sz=_sz(o))

        def stt(o, a, s_, b, op0, op1, reads, writes):
            P.add("vector", lambda e: e.scalar_tensor_tensor(out=o, in0=a, scalar=s_, in1=b, op0=op0, op1=op1), reads=reads, writes=writes, sz=_sz(o))

        def vcopy(o, a, reads, writes):
            P.add("vector", lambda e: e.tensor_copy(out=o, in_=a), reads=reads, writes=writes, sz=_sz(o))

        def acopy(o, a, reads, writes):
            P.add("scalar", lambda e: e.copy(out=o, in_=a), reads=reads, writes=writes, sz=_sz(o))

        def vmemset(o, val, writes):
            P.add("vector", lambda e: e.memset(o, val), writes=writes)

        def dbg_dump(name, src_ap, rkeys):
            if dbg_out is not None and name in dbg_out:
                dma(dbg_out[name], src_ap, rkeys, [("dbg", name)], q="gpsimd")

        HmK = [("Hm", c) for c in range(8)]
        HfK = [("Hf", c) for c in range(8)]
        U16K = [("U16", c) for c in range(8)]
        xTK = [("xT", c) for c in range(8)]

        dma(consts[:], const_in, [], ["consts"])
        for l in range(L):
            dma(vec[:, l, :], vec_in[l], [], [("vec", l)])
            dma(bvb[:, l, :], bv_in[l].partition_broadcast(128), [], [("bv", l)])
        dma(cact[:], c_in, [], ["cact"])
        vcopy(identb[:], ident, ["consts"], ["identb"])
        vcopy(trib[:], tri, ["consts"], ["trib"])
        vcopy(onesb[:], onesf, ["consts"], ["onesb"])
        act(cact[:], cact[:], AF.Silu, ["cact"], ["cact"])

        NSPC = WPAD // 1024
        cvn = {"n": 0}
        converted = set()

        def conv_subpiece(sp, q_in="sync", q_out="sync"):
            converted.add(sp)
            i_ = cvn["n"] % 4
            cvn["n"] += 1
            dma(stg[i_], w_in[:, sp * 1024:(sp + 1) * 1024], [], stgk[i_], q=q_in)
            if cvn["n"] % 2 == 0:
                vcopy(stgb[i_], stg[i_], stgk[i_], stgbk[i_])
            else:
                acopy(stgb[i_], stg[i_], stgk[i_], stgbk[i_])
            dma(wb_d[:, sp * 1024:(sp + 1) * 1024], stgb[i_], stgbk[i_], [("wbf", sp)], q=q_out)

        n_early = (WB[("UP", 0)][0] + 1023) // 1024
        for sp in range(n_early):
            conv_subpiece(sp)
        rowbuf = recip[0:1, :]
        vmemset(V[:, 0:16, :, 64:65], 1.0, [("V", q) for q in range(16)])
        vmemset(V[:, 24:32, :, 64:65], 1.0, [("V", q) for q in range(24, 32)])

        def mod_block(l, blk, bank, q):
            dma(U16[:, :, 0:512], modw_in[l * 12 + blk].rearrange("p (k n) -> p k n", k=8), [], U16K, q=q)
            for k in range(8):
                mm(pb[bank][0:1, :], cact[:, k:k + 1], U16[:, k, 0:512], k == 0, k == 7, [("U16", k), "cact"], [("ps", bank)])
            vcopy(rowbuf[:], pb[bank][0:1, :], [("ps", bank)], ["recip"])
            for jj in range(4):
                mm(pb[bank][:, jj:jj + 1], rowbuf[0:1, jj * 128:(jj + 1) * 128], onesf[0:1, 0:1], True, True,
                   ["recip", "consts"], [("ps", bank)])
            tt(modv[:, l, blk * 4:blk * 4 + 4], pb[bank][:, 0:4], vec[:, l, VC["modb"] + blk * 4:VC["modb"] + blk * 4 + 4], ALU.add,
               [("ps", bank), ("vec", l)], [("modv", l)])

        def mod_finish(l):
            stt(gs[:, l, 0:8], modv[:, l, 8:16], 1.0, vec[:, l, VC["nmg"]:VC["nmg"] + 8], ALU.add, ALU.mult,
                [("modv", l), ("vec", l)], [("gs", l)])
            stt(gs[:, l, 8:16], modv[:, l, 32:40], 1.0, vec[:, l, VC["nfg"]:VC["nfg"] + 8], ALU.add, ALU.mult,
                [("modv", l), ("vec", l)], [("gs", l)])
            ts(hgf[:, l, :], modv[:, l, 40:48], 0.5, None, ALU.mult, None, [("modv", l)], [("gs", l)])

        for blk in range(12):
            mod_block(0, blk, 5, "sync")
        mod_finish(0)
        defer_mod = [(1, blk) for blk in range(12)] if n_layers > 1 else []

        wst = {"M": 0, "F": 0}
        bg_sp = []

        def need_conv(sp_lo, sp_hi):
            bgl = None
            while bgl and any(sp_ not in converted for sp_ in range(sp_lo, sp_hi + 1)):
                if bg_sp:
                    bg_sp.pop(0)
                P._in_bg = True
                bgl.pop(0)()
                P._in_bg = False

        def wload(stream, l, key):
            ring = ringM if stream == "M" else ringF
            off, n = WB[key]
            slot = wst[stream] % len(ring)
            wst[stream] += 1
            a_ = l * WL + off
            need_conv(a_ // 1024, (a_ + n - 1) // 1024)
            dma(ring[slot][:, 0:n], wb_d[:, a_:a_ + n], [("wbf", i) for i in range(a_ // 1024, (a_ + n - 1) // 1024 + 1)],
                [("wr" + stream, slot)], q=("gpsimd" if (stream == "F" and os.environ.get("KFQ")) else "sync"))
            return ring[slot], ("wr" + stream, slot)

        sqn = {"n": 0}

        def sq_acc(src, skey, bank, first, last, side):
            i = sqn["n"]
            sqn["n"] += 1
            if side == "M":
                dst, dk = ((xdt, "xdt"), (xdte, "xdte"))[i % 2]
                dst = dst[:]
            else:
                dst, dk = sq[:, i % 2, :], ("sq", i % 2)
            if i % 3 == 2:
                ptt(dst, src, src, ALU.mult, [skey], [dk])
            else:
                tt(dst, src, src, ALU.mult, [skey], [dk])
            mm(pb[bank][:], onesb[:], dst, first, last, [dk, "onesb"], [("ps", bank)])

        def rstd_from(dst, bank, nfeat, wkey, eps=EPS):
            act(dst, pb[bank][:], AF.Ln, [("ps", bank)], [wkey], bias=eps, scale=1.0 / nfeat)
            act(dst, dst, AF.Exp, [wkey], [wkey], scale=-0.5)

        nbM = {"n": 0}

        def mbank():
            b = nbM["n"] % 4
            nbM["n"] += 1
            return b

        def emit_M(l, j):
            t0 = j * T
            if j == 0:
                a_ = l * WL + WB[("INS",)][0]
                need_conv(a_ // 1024, (a_ + 127) // 1024)
                dma(wS[:], wb_d[:, a_:a_ + 128].rearrange("p (k n) -> p k n", k=8),
                    [("wbf", i) for i in range(a_ // 1024, (a_ + 127) // 1024 + 1)], ["wS"])
                dma(gbc[:], grow_in[l].partition_broadcast(128), [], ["gbc"], q="gpsimd")
                ts(hcw[:], vec[:, l, VC["cw"]:VC["cw"] + 40], 0.5, None, ALU.mult, None, [("vec", l)], ["hcw"])
                act(aneg[:], bvb[:, l, 16:24], AF.Exp, [("bv", l)], ["aneg"])
                ts(aneg[:], aneg[:], -1.0, None, ALU.mult, None, ["aneg"], ["aneg"])
                vcopy(Dbc[:], bvb[:, l, 24:32], [("bv", l)], ["Dbc"])
                vmemset(carryF[:], 0.0, ["carryF"])
                vmemset(Sst[:], 0.0, ["Sst"])
                vmemset(Sbf[:], 0.0, ["Sbf"])
            if defer_mod and ((l == 0 and j >= 1) or l == 1):
                for _ in range(2 if l == 0 else 12):
                    if defer_mod:
                        mod_block(*defer_mod.pop(0), 5, "gpsimd")
                if not defer_mod:
                    mod_finish(1)
            if l == 0:
                for blk in range(4):
                    if blk % 2 == 0:
                        xtok = PT[:].rearrange("p a b c -> p (a b c)").bitcast(F32)
                        hkeys = [("PT", 0, 0), ("PT", 0, 1), ("PT", 1, 0), ("PT", 1, 1)]
                    else:
                        xtok = qT[:].rearrange("p a b -> p (a b)").bitcast(F32)
                        hkeys = [("qT", m_) for m_ in range(4)]
                    dma(xtok, x_in[t0 + blk * 128:t0 + (blk + 1) * 128, :], [], hkeys, q="gpsimd")
                    for half in range(2):
                        bank = mbank()
                        for q in range(4):
                            c = half * 4 + q
                            tr(pb[bank][:, q * 128:(q + 1) * 128], xtok[:, c * 128:(c + 1) * 128], ident,
                               hkeys + ["consts"], [("ps", bank)])
                        vcopy(U16[:, half * 4:half * 4 + 4, blk * 128:(blk + 1) * 128],
                              pb[bank][:].rearrange("p (q t) -> p q t", q=4), [("ps", bank)],
                              [("U16", half * 4 + q) for q in range(4)])
                dma(x0T[:, :, t0:t0 + T].rearrange("c p t -> p c t"), U16[:, :, 0:512], U16K, [("x0T", j)], q="gpsimd")
            else:
                dma(U16[:, :, 0:512], x1T[:, :, t0:t0 + T].rearrange("c p t -> p c t"), [("x1T", j)], U16K, q="gpsimd")
            for c in range(8):
                sq_acc(U16[:, c, 0:512], ("U16", c), 4, c == 0, c == 7, "M")
            rstd_from(rbc[:], 4, D, "rbc")
            for c in range(8):
                bank = mbank()
                g_ = gs[:, l, c:c + 1]
                sh_ = modv[:, l, c:c + 1]
                tt(pb[bank][:], U16[:, c, 0:512], rbc[:], ALU.mult, [("U16", c), "rbc"], [("ps", bank)])
                if c % 2 == 0:
                    act(Hm[:, c, :], pb[bank][:], AF.Identity, [("ps", bank), ("gs", l), ("modv", l)], [("Hm", c)], bias=sh_, scale=g_)
                else:
                    ts(Hm[:, c, :], pb[bank][:], g_, sh_, ALU.mult, ALU.add, [("ps", bank), ("gs", l), ("modv", l)], [("Hm", c)])
            for c in range(8):
                if j == 0:
                    vmemset(U16[:, c, 0:3], 0.0, [("U16", c)])
                else:
                    vcopy(U16[:, c, 0:3], U16[:, c, 512:515], [("U16", c)], [("U16", c)])
            for cb in range(10):
                ring, rk = wload("M", l, ("INF", cb))
                w3 = ring[:, 0:2048].rearrange("p (k n) -> p k n", k=8)
                for m2 in range(2):
                    bank = mbank()
                    m = (cb % 2) * 2 + m2
                    for k in range(8):
                        mm(pb[bank][:], w3[:, k, m2 * 128:(m2 + 1) * 128], Hm[:, k, :], k == 0, k == 7,
                           [rk, ("Hm", k)], [("ps", bank)])
                    grp = cb // 2
                    if grp == 0:
                        acopy(qT[:, m, :], pb[bank][:], [("ps", bank)], [("qT", m)])
                    elif grp == 1:
                        vcopy(KT[:, m, t0:t0 + T], pb[bank][:], [("ps", bank)], [("KT", m, j)])
                    elif grp == 2:
                        act(zs[:, m, :], pb[bank][:], AF.Tanh, [("ps", bank)], [("zs", m)], scale=0.5)
                        stt(zs[:, m, :], zs[:, m, :], 1.0, pb[bank][:], ALU.add, ALU.mult, [("zs", m), ("ps", bank)], [("zs", m)])
                    else:
                        c = (grp - 3) * 4 + m
                        if m % 2 == 0:
                            vcopy(U16[:, c, 3:515], pb[bank][:], [("ps", bank)], [("U16", c)])
                        else:
                            acopy(U16[:, c, 3:515], pb[bank][:], [("ps", bank)], [("U16", c)])
            vr = []
            for cb in range(2):
                vr.append(wload("M", l, ("INV", cb)))
            for blk in range(4):
                bank = mbank()
                for cb in range(2):
                    ring, rk = vr[cb]
                    w3 = ring[:, 0:2048].rearrange("p (k n) -> p k n", k=8)
                    for k in range(8):
                        mm(pb[bank][:, cb * 256:(cb + 1) * 256], Hm[:, k, blk * 128:(blk + 1) * 128], w3[:, k, :], k == 0, k == 7,
                           [rk, ("Hm", k)], [("ps", bank)])
                if blk % 2 == 0:
                    vcopy(V[:, 4 * j + blk, :, 0:64], pb[bank][:].rearrange("p (h d) -> p h d", h=8), [("ps", bank)], [("V", 4 * j + blk)])
                else:
                    acopy(V[:, 4 * j + blk, :, 0:64], pb[bank][:].rearrange("p (h d) -> p h d", h=8), [("ps", bank)], [("V", 4 * j + blk)])
            for blk in range(4):
                for k in range(8):
                    mm(pb[5][:, blk * 16:(blk + 1) * 16], Hm[:, k, blk * 128:(blk + 1) * 128], wS[:, k, :], k == 0, k == 7,
                       [("Hm", k), "wS"], [("ps", 5)])
            ps16 = pb[5][:, 0:64].rearrange("p (b n) -> p b n", b=4)
            tt(sm16[:], ps16, bvb[:, l, 0:16].unsqueeze(1).broadcast_to([128, 4, 16]), ALU.add, [("ps", 5), ("bv", l)], ["sm16"])
            tt(sm16[:], sm16[:], sgn.unsqueeze(1).broadcast_to([128, 4, 16]), ALU.mult, ["sm16", "consts"], ["sm16"])
            act(sm16[:], sm16[:], AF.Exp, ["sm16"], ["sm16"])
            act(sm16[:], sm16[:], AF.Ln, ["sm16"], ["sm16"], bias=1.0)
            vcopy(a16[:, :, 0:8], sm16[:, :, 0:8], ["sm16"], ["a16"])
            tt(a16[:, :, 8:16], sm16[:, :, 8:16], aneg[:].unsqueeze(1).broadcast_to([128, 4, 8]), ALU.mult, ["sm16", "aneg"], ["a16"])
            for blk in range(4):
                mm(pb[4][:, blk * 32:blk * 32 + 16], tri, a16[:, blk, :], True, True, ["consts", "a16"], [("ps", 4)])
                mm(pb[4][:, blk * 32 + 16:blk * 32 + 32], onesf, a16[:, blk, :], True, True, ["consts", "a16"], [("ps", 4)])
            cs4 = pb[4][:, 0:128].rearrange("p (b n) -> p b n", b=4)
            vcopy(acs[:], cs4[:, :, 0:16], [("ps", 4)], ["acs"])
            vcopy(tot16[:], cs4[:, :, 16:32], [("ps", 4)], ["tot16"])
            for blk in range(4):
                gb = 4 * j + blk
                tt(F_all[:, gb, :], acs[:, blk, 0:8], carryF[:], ALU.add, ["acs", "carryF"], [("F_all", gb)])
                tt(carryF[:], carryF[:], tot16[:, blk, 0:8], ALU.add, ["carryF", "tot16"], ["carryF"])
                if blk == 0:
                    vcopy(fref[:, 0, :], carryF[:], ["carryF"], ["fref"])
                if blk == 2:
                    vcopy(fref[:, 1, :], carryF[:], ["carryF"], ["fref"])
            nkb = 4 * j + 4
            for half in range(2):
                tt(biasT[:, half, 0:nkb, :], F_all[:, 0:nkb, :], fref[:, half, :].unsqueeze(1).broadcast_to([128, nkb, 8]),
                   ALU.subtract, [("F_all", g) for g in range(nkb)] + ["fref"], ["biasT"])

            P.add("MARK", None)
            rec = []
            _radd = P.add
            P.add = lambda *a_, **k_: rec.append((a_, k_))
            X, Y = 4, 5
            for c in range(8):
                cwv = lambda k: hcw[:, k * 8 + c:k * 8 + c + 1]
                bk = X + c % 2
                acc = pb[bk][:]
                ts(acc, U16[:, c, 0:512], cwv(0), hcw[:, 32 + c:33 + c], ALU.mult, ALU.add,
                   [("U16", c), "hcw"], [("ps", bk)])
                for k in range(1, 4):
                    stt(acc, U16[:, c, k:k + 512], cwv(k), acc, ALU.mult, ALU.add, [("U16", c), "hcw", ("ps", bk)], [("ps", bk)])
                if c < 4:
                    dst_, dk_ = xsTb[:, c, :], ("xsTb", c)
                elif c < 6:
                    dst_, dk_ = BTb[:, c - 4, :], ("BTb", c - 4)
                else:
                    dst_, dk_ = CTb[:, c - 6, :], ("CTb", c - 6)
                act(dst_, acc, AF.Tanh, [("ps", bk)], [dk_])
                stt(dst_, dst_, 1.0, acc, ALU.add, ALU.mult, [dk_, ("ps", bk)], [dk_])
            sm = small
            for blk in range(4):
                tk = slice(blk * 128, (blk + 1) * 128)
                for c in range(4):
                    mm(pb[X][:, c * 128:(c + 1) * 128], xsTb[:, c, tk], identb[:], True, True, [("xsTb", c), "identb"], [("ps", X)])
                for g in range(2):
                    mm(pb[Y][:, g * 128:(g + 1) * 128], BTb[:, g, tk], identb[:], True, True, [("BTb", g), "identb"], [("ps", Y)])
                for g in range(2):
                    mm(pb[Y][:, 256 + g * 128:256 + (g + 1) * 128], BTb[:, g, tk], CTb[:, g, tk], True, True,
                       [("BTb", g), ("CTb", g)], [("ps", Y)])
                tt(sm[:, 0, :], tot16[:, blk, 8:16], acs[:, blk, 8:16], ALU.subtract, ["tot16", "acs"], ["small"])
                act(sm[:, 0, :], sm[:, 0, :], AF.Exp, ["small"], ["small"])
                act(sm[:, 1, :], acs[:, blk, 8:16], AF.Exp, ["acs", "small"], ["small"])
                act(sm[:, 2, :], tot16[:, blk, 8:16], AF.Exp, ["tot16", "small"], ["small"])
                tt(sm[:, 3, :], sm[:, 0, :], sm16[:, blk, 8:16], ALU.mult, ["small", "sm16"], ["small"])
                ts(sm[:, 4, :], acs[:, blk, 8:16], -1.0, None, ALU.mult, None, ["acs", "small"], ["small"])
                ps3 = pb[X][:].rearrange("p (h d) -> p h d", h=8)
                vcopy(xs_tok[:], pb[X][:], [("ps", X)], ["xs_tok"])
                tt(xdt[:].rearrange("p (h d) -> p h d", h=8), ps3, sm16[:, blk, 8:16].unsqueeze(2).broadcast_to([128, 8, 64]), ALU.mult,
                   [("ps", X), "sm16"], ["xdt"])
                tt(xdte[:].rearrange("p (h d) -> p h d", h=8), ps3, sm[:, 3, :].unsqueeze(2).broadcast_to([128, 8, 64]), ALU.mult,
                   [("ps", X), "small"], ["xdte"])
                acopy(B_tok[:], pb[Y][:, 0:256], [("ps", Y)], ["B_tok"])
                for hh in range(2):
                    for hq in range(4):
                        h = hh * 4 + hq
                        o_ = pb[X][:, hq * 128:(hq + 1) * 128]
                        mm(o_, acs[:, blk, 8 + h:9 + h].broadcast_to([128, 128]), ident, True, False, ["acs", "consts"], [("ps", X)])
                        mm(o_, ident, negm, False, True, ["consts"], [("ps", X)])
                    for hq in range(4):
                        h = hh * 4 + hq
                        act(MT[:, h, :], pb[X][:, hq * 128:(hq + 1) * 128], AF.Exp, [("ps", X), "small"], [("MT", h // 4)],
                            bias=sm[:, 4, h:h + 1])
                for g in range(2):
                    tt(MT[:, 4 * g:4 * g + 4, :], MT[:, 4 * g:4 * g + 4, :],
                       pb[Y][:, 256 + g * 128:256 + (g + 1) * 128].unsqueeze(1).broadcast_to([128, 4, 128]), ALU.mult,
                       [("MT", g), ("ps", Y)], [("MT", g)])
                for h in range(8):
                    mm(pb[X][:, h * 64:(h + 1) * 64], MT[:, h, :], xdt[:, h * 64:(h + 1) * 64], True, True,
                       [("MT", h // 4), "xdt"], [("ps", X)])
                for g in range(2):
                    mm(pb[Y][:, g * 256:(g + 1) * 256], CTb[:, g, tk], Sbf[:, g, :], True, True, [("CTb", g), "Sbf"], [("ps", Y)])
                yt3 = ytok[:].rearrange("p (h d) -> p h d", h=8)
                tt(yt3, pb[Y][:].rearrange("p (h d) -> p h d", h=8), sm[:, 1, :].unsqueeze(2).broadcast_to([128, 8, 64]), ALU.mult,
                   [("ps", Y), "small"], ["rbc"])
                tt(ytok[:], ytok[:], pb[X][:], ALU.add, ["rbc", ("ps", X)], ["rbc"])
                tt(pb[Y][:].rearrange("p (h d) -> p h d", h=8), xs_tok[:].rearrange("p (h d) -> p h d", h=8),
                   Dbc[:].unsqueeze(2).broadcast_to([128, 8, 64]), ALU.mult, ["xs_tok", "Dbc", ("ps", Y)], [("ps", Y)])
                tt(ytok[:], ytok[:], pb[Y][:], ALU.add, ["rbc", ("ps", Y)], ["rbc"])
                for g in range(2):
                    mm(pb[Y][:, g * 256:(g + 1) * 256], B_tok[:, g * 128:(g + 1) * 128], xdte[:, g * 256:(g + 1) * 256], True, True,
                       ["B_tok", "xdte"], [("ps", Y)])
                for g in range(2):
                    s3 = Sst[:, g, :].rearrange("p (r d) -> p r d", r=4)
                    tt(s3, s3, sm[:, 2, 4 * g:4 * g + 4].unsqueeze(2).broadcast_to([128, 4, 64]), ALU.mult, ["Sst", "small"], ["Sst"])
                    tt(Sst[:, g, :], Sst[:, g, :], pb[Y][:, g * 256:(g + 1) * 256], ALU.add, ["Sst", ("ps", Y)], ["Sst"])
                vcopy(Sbf[:], Sst[:], ["Sst"], ["Sbf"])
                for c in range(4):
                    tr(pb[X][:, c * 128:(c + 1) * 128], ytok[:, c * 128:(c + 1) * 128], ident, ["rbc", "consts"], [("ps", X)])
                for c in range(4):
                    tt(U16[:, 4 + c, blk * 128:(blk + 1) * 128], pb[X][:, c * 128:(c + 1) * 128], zs[:, c, tk], ALU.mult,
                       [("ps", X), ("zs", c)], [("U16", 4 + c)])
            P.add = _radd

            steps = [(hp, kb) for hp in range(4) for kb in range(nkb)]

            def qk(i):
                hp, kb = steps[i]
                m = kb - 4 * j
                c0 = 128 * m if m > 0 else 0
                ks = slice(kb * 128, (kb + 1) * 128)
                mm(pb[0][:, c0:512], KT[0:64, hp, ks], qT[0:64, hp, c0:512], True, True, [("KT", hp, kb // 4), ("qT", hp)], [("ps", 0)])
                mm(pb[1][:, c0:512], KT[64:128, hp, ks], qT[64:128, hp, c0:512], True, True, [("KT", hp, kb // 4), ("qT", hp)], [("ps", 1)])

            def softmax(i):
                hp, kb = steps[i]
                m = kb - 4 * j
                c0 = 128 * m if m > 0 else 0
                pbuf = i % 2
                for (hd, bk, xi) in ((2 * hp, 0, 0), (2 * hp + 1, 1, 1)):
                    for half in range(2):
                        lo = max(c0, 256 * half)
                        hi = 256 * (half + 1)
                        if lo >= hi:
                            continue
                        act(PT[:, pbuf, xi, lo:hi], pb[bk][:, lo:hi], AF.Exp, [("ps", bk), "biasT"],
                            [("PT", pbuf, xi)], bias=biasT[:, half, kb, hd:hd + 1], scale=0.125)
                    if m >= 0:
                        tt(PT[:, pbuf, xi, 128 * m:128 * m + 128], PT[:, pbuf, xi, 128 * m:128 * m + 128], trib[:], ALU.mult,
                           [("PT", pbuf, xi), "trib"], [("PT", pbuf, xi)])

            def pv(i):
                hp, kb = steps[i]
                m = kb - 4 * j
                pbuf = i % 2
                for xi in range(2):
                    h = 2 * hp + xi
                    bank = 2 + xi
                    for tc in range(4):
                        if m > tc:
                            continue
                        first = (kb == 0 and tc == 0)
                        P.add("tensor", lambda e, bank=bank, tc=tc, xi=xi, h=h, first=first, kb=kb, pbuf=pbuf: e.matmul(
                            pb[bank][:, tc * 65:(tc + 1) * 65], lhsT=PT[:, pbuf, xi, tc * 128:(tc + 1) * 128], rhs=V[:, kb, h, :],
                            start=first, stop=(kb == 4 * j + tc), skip_group_check=True),
                            reads=[("V", kb), ("PT", pbuf, xi)], writes=[("ps", bank)])
                if kb == nkb - 1:
                    for xi in range(2):
                        h = 2 * hp + xi
                        bank = 2 + xi
                        o3 = pb[bank][:, 0:260].rearrange("p (t d) -> p t d", t=4)
                        P.add("vector", lambda e, o3=o3, xi=xi: e.reciprocal(out=rsm[:, 4 * xi:4 * xi + 4].unsqueeze(2), in_=o3[:, :, 64:65]),
                              reads=[("ps", bank)], writes=["rsm"])
                        tt(U16[:, 0:4, h * 64:(h + 1) * 64], o3[:, :, 0:64], rsm[:, 4 * xi:4 * xi + 4].unsqueeze(2).broadcast_to([128, 4, 64]),
                           ALU.mult, [("ps", bank), "rsm"], [("U16", c) for c in range(4)])

            per_step = (len(rec) + len(steps) - 1) // len(steps)
            ri = 0
            qk(0)
            for i in range(len(steps)):
                softmax(i)
                if i + 1 < len(steps):
                    qk(i + 1)
                pv(i)
                for (a_, k_) in rec[ri:ri + per_step]:
                    P.add(*a_, **k_)
                ri += per_step
            for (a_, k_) in rec[ri:]:
                P.add(*a_, **k_)
            if l == 0 and j == 0:
                dbg_dump("ymix", U16[:, :, 0:512], U16K)

            ssq = small[:, 5, 0:4]
            for tc in range(4):
                bk = mbank()
                P.add("scalar", lambda e, tc=tc, bk=bk: e.activation(
                    out=pb[bk][:], in_=U16[:, tc, 0:512], func=AF.Square, accum_out=small[:, 5, tc:tc + 1]),
                    reads=[("U16", tc)], writes=[("ps", bk), "small"])
            act(ssq, ssq, AF.Ln, ["small"], ["small"], bias=EPS, scale=1.0 / 512)
            act(ssq, ssq, AF.Exp, ["small"], ["small"], scale=-0.5)
            for tc in range(4):
                stt(U16[:, tc, 0:512], U16[:, tc, 0:512], small[:, 5, tc:tc + 1], gbc[:], ALU.mult, ALU.mult,
                    [("U16", tc), "small", "gbc"], [("U16", tc)])
            for tc in range(4):
                bk = mbank()
                for c in range(4):
                    tr(pb[bk][:, c * 128:(c + 1) * 128], U16[:, tc, c * 128:(c + 1) * 128], ident, [("U16", tc), "consts"], [("ps", bk)])
                if tc % 2 == 0:
                    vcopy(Hm[:, 0:4, tc * 128:(tc + 1) * 128], pb[bk][:].rearrange("p (c t) -> p c t", c=4), [("ps", bk)],
                          [("Hm", c) for c in range(4)])
                else:
                    acopy(Hm[:, 0:4, tc * 128:(tc + 1) * 128], pb[bk][:].rearrange("p (c t) -> p c t", c=4), [("ps", bk)],
                          [("Hm", c) for c in range(4)])
            for (c_lo, nch, gname) in ((4, 2, "sng"), (6, 2, "sng")):
                for i in range(nch):
                    sq_acc(U16[:, c_lo + i, 0:512], ("U16", c_lo + i), 4, i == 0, i == nch - 1, "M")
                rstd_from(rbcY[:], 4, 128 * nch, "recip", eps=4.0 * EPS)
                for i in range(nch):
                    c = c_lo + i
                    gi = c - 4
                    stt(Hm[:, c, :], U16[:, c, 0:512], vec[:, l, VC[gname] + gi:VC[gname] + gi + 1], rbcY[:], ALU.mult, ALU.mult,
                        [("U16", c), ("vec", l), "recip"], [("Hm", c)])

        TAILF = bool(int(os.environ.get("KTAILF", "1")))

        def emit_tail(l, j):
            t0 = j * T
            src = x0T if l == 0 else x1T
            dma(xT[:], src[:, :, t0:t0 + T].rearrange("c p t -> p c t"), [("x0T" if l == 0 else "x1T", j)], xTK, q="gpsimd")
            for cb in range(4):
                ring, rk = wload("M", l, ("OUT", cb))
                w3 = ring[:, 0:2048].rearrange("p (k n) -> p k n", k=8)
                for m2 in range(2):
                    c = cb * 2 + m2
                    bank = (6 + c % 2) if TAILF else mbank()
                    for k in range(8):
                        mm(pb[bank][:], w3[:, k, m2 * 128:(m2 + 1) * 128], Hm[:, k, :], k == 0, k == 7, [rk, ("Hm", k)], [("ps", bank)])
                    stt(xT[:, c, :], pb[bank][:], modv[:, l, 16 + c:17 + c], xT[:, c, :], ALU.mult, ALU.add,
                        [("ps", bank), ("modv", l), ("xT", c)], [("xT", c)])
                    if not TAILF:
                        sq_acc(xT[:, c, :], ("xT", c), 6, c == 0, c == 7, "F")
            if TAILF:
                for c in range(8):
                    sq_acc(xT[:, c, :], ("xT", c), 6, c == 0, c == 7, "F")
            if l == 0 and j == 0:
                dbg_dump("xmid", xT[:], xTK)

        def emit_F(l, j):
            t0 = j * T
            if j == 0:
                vmemset(uhalo[:], 0.0, ["uhalo"])
            act(pb[7][:], pb[6][:], AF.Ln, [("ps", 6)], [("ps", 7)], bias=EPS, scale=1.0 / D)
            act(pb[7][:], pb[7][:], AF.Exp, [("ps", 7)], [("ps", 7)], scale=-0.5)
            for c in range(8):
                tb = tmp[:, c % 2, :]
                tk_ = ("tmp", c % 2)
                g_ = gs[:, l, 8 + c:9 + c]
                sh_ = modv[:, l, 24 + c:25 + c]
                tt(tb, xT[:, c, :], pb[7][:], ALU.mult, [("xT", c), ("ps", 7)], [tk_])
                if c % 2 == 0:
                    act(Hf[:, c, :], tb, AF.Identity, [tk_, ("gs", l), ("modv", l)], [("Hf", c)], bias=sh_, scale=g_)
                else:
                    ts(Hf[:, c, :], tb, g_, sh_, ALU.mult, ALU.add, [tk_, ("gs", l), ("modv", l)], [("Hf", c)])
            for gi in range(22):
                ring, rk = wload("F", l, ("UP", gi))
                w3 = ring[:, 0:2048].rearrange("p (k n) -> p k n", k=8)
                for u in range(2):
                    bank = 6 + u
                    for k in range(8):
                        mm(pb[bank][:], w3[:, k, u * 128:(u + 1) * 128], Hf[:, k, :], k == 0, k == 7, [rk, ("Hf", k)], [("ps", bank)])
                for (u, idx) in ((0, gi), (1, 22 + gi)):
                    bank = 6 + u
                    fw = lambda k, idx=idx: vec[:, l, VC["fcw"] + k * 44 + idx:VC["fcw"] + k * 44 + idx + 1]
                    kf = os.environ.get("KF_ACT", "0")
                    if kf == "1":
                        acopy(ubuf[:, u, 2:514], pb[bank][:], [("ps", bank)], [("ubuf", u)])
                        act(tmp[:, u, :], pb[bank][:], AF.Identity, [("ps", bank), ("vec", l)], [("tmp", u)],
                            bias=vec[:, l, VC["fcb"] + idx:VC["fcb"] + idx + 1], scale=fw(2))
                    elif kf == "2":
                        acopy(ubuf[:, u, 2:514], pb[bank][:], [("ps", bank)], [("ubuf", u)])
                        ts(tmp[:, u, :], pb[bank][:], fw(2), vec[:, l, VC["fcb"] + idx:VC["fcb"] + idx + 1], ALU.mult, ALU.add,
                           [("ps", bank), ("vec", l)], [("tmp", u)])
                    elif kf == "4" and u == 0:
                        acopy(ubuf[:, u, 2:514], pb[bank][:], [("ps", bank)], [("ubuf", u)])
                        act(tmp[:, u, :], pb[bank][:], AF.Identity, [("ps", bank), ("vec", l)], [("tmp", u)],
                            bias=vec[:, l, VC["fcb"] + idx:VC["fcb"] + idx + 1], scale=fw(2))
                    elif kf == "3":
                        vcopy(ubuf[:, u, 2:514], pb[bank][:], [("ps", bank)], [("ubuf", u)])
                        act(tmp[:, u, :], pb[bank][:], AF.Identity, [("ps", bank), ("vec", l)], [("tmp", u)],
                            bias=vec[:, l, VC["fcb"] + idx:VC["fcb"] + idx + 1], scale=fw(2))
                    else:
                        vcopy(ubuf[:, u, 2:514], pb[bank][:], [("ps", bank)], [("ubuf", u)])
                        ts(tmp[:, u, :], pb[bank][:], fw(2), vec[:, l, VC["fcb"] + idx:VC["fcb"] + idx + 1], ALU.mult, ALU.add,
                           [("ps", bank), ("vec", l)], [("tmp", u)])
                    vcopy(ubuf[:, u, 0:2], uhalo[:, idx, :], ["uhalo", ("ubuf", u)], [("ubuf", u)])
                    vcopy(uhalo[:, idx, :], ubuf[:, u, 512:514], [("ubuf", u)], ["uhalo"])
                    for k in range(0, 2):
                        stt(tmp[:, u, :], ubuf[:, u, k:k + 512], fw(k), tmp[:, u, :], ALU.mult, ALU.add,
                            [("ubuf", u), ("vec", l), ("tmp", u)], [("tmp", u)])
                act(ubuf[:, 0, 0:512], tmp[:, 0, :], AF.Tanh, [("tmp", 0), ("ubuf", 0)], [("ubuf", 0)], scale=0.5)
                stt(tmp[:, 0, :], ubuf[:, 0, 0:512], 1.0, tmp[:, 0, :], ALU.add, ALU.mult, [("ubuf", 0), ("tmp", 0)], [("tmp", 0)])
                ptt(gT[:, gi, :], tmp[:, 0, :], tmp[:, 1, :], ALU.mult, [("tmp", 0), ("tmp", 1)], [("gT", gi)])
            for c in range(8):
                bank = 6 + c % 2
                for kh in range(2):
                    ring, rk = wload("F", l, ("DOWN", 2 * c + kh))
                    w3 = ring[:, 0:1408].rearrange("p (k n) -> p k n", k=11)
                    for k in range(11):
                        kk = kh * 11 + k
                        mm(pb[bank][:], w3[:, k, :], gT[:, kk, :], kk == 0, kk == 21, [rk, ("gT", kk)], [("ps", bank)])
                stt(xT[:, c, :], pb[bank][:], hgf[:, l, c:c + 1], xT[:, c, :], ALU.mult, ALU.add,
                    [("ps", bank), ("gs", l), ("xT", c)], [("xT", c)])
            if l == 0 and j == 0:
                dbg_dump("xout", xT[:], xTK)
            if l < L - 1:
                dma(x1T[:, :, t0:t0 + T].rearrange("c p t -> p c t"), xT[:], xTK, [("x1T", j)], q="gpsimd")
            else:
                for c in range(8):
                    sq_acc(xT[:, c, :], ("xT", c), 6, c == 0, c == 7, "F")
                act(pb[7][:], pb[6][:], AF.Ln, [("ps", 6)], [("ps", 7)], bias=EPS, scale=1.0 / D)
                act(pb[7][:], pb[7][:], AF.Exp, [("ps", 7)], [("ps", 7)], scale=-0.5)
                for c in range(8):
                    stt(xT[:, c, :], xT[:, c, :], vec[:, l, VC["fin"] + c:VC["fin"] + c + 1], pb[7][:], ALU.mult, ALU.mult,
                        [("xT", c), ("vec", l), ("ps", 7)], [("xT", c)])
                nbf = 0
                for blk in range(4):
                    hs = blk % 2
                    otok = Hf[:, 4 * hs:4 * hs + 4, :].rearrange("p a b -> p (a b)").bitcast(F32)
                    hkeys = [("Hf", 4 * hs + i) for i in range(4)]
                    for half in range(2):
                        bank = 6 + nbf % 2
                        nbf += 1
                        for q in range(4):
                            c = half * 4 + q
                            tr(pb[bank][:, q * 128:(q + 1) * 128], xT[:, c, blk * 128:(blk + 1) * 128], ident,
                               [("xT", c), "consts"], [("ps", bank)])
                        vcopy(otok[:, half * 512:(half + 1) * 512], pb[bank][:], [("ps", bank)], hkeys)
                    dma(out[t0 + blk * 128:t0 + (blk + 1) * 128, :], otok, hkeys, [("out", t0 + blk * 128)], q="gpsimd")

        def record(fn, *a):
            r = []
            _radd = P.add
            P.add = lambda *a_, **k_: r.append((a_, k_))
            fn(*a)
            P.add = _radd
            return r

        tiles = [(l, j) for l in range(n_layers) for j in range(n_tiles)]
        P.bg = [(lambda sp=sp: conv_subpiece(sp, "gpsimd", "gpsimd")) for sp in range(n_early, NSPC)]
        bg_sp.extend(range(n_early, NSPC))
        bg_sp.append(10 ** 9)
        P.bg.append(lambda: vmemset(V[:, 16:24, :, 64:65], 1.0, [("V", q) for q in range(16, 24)]))
        P.bg_every = int(os.environ.get("KBG", "48"))
        BUR = int(os.environ.get("KBUR", "2"))

        P.bg_sp = bg_sp

        def flush_bg():
            while P.bg:
                if bg_sp:
                    bg_sp.pop(0)
                P._in_bg = True
                P.bg.pop(0)()
                P._in_bg = False

        def tail_and_F(l, j):
            emit_tail(l, j)
            emit_F(l, j)

        emit_M(*tiles[0])
        if not TAILF:
            emit_tail(*tiles[0])
        for g, tl in enumerate(tiles):
            if g + 1 < len(tiles) and merge:
                if tiles[g + 1] == (0, 4):
                    flush_bg()
                rF = record(emit_F, *tl)
                rM = record(emit_M, *tiles[g + 1])
                rM = [x_ for x_ in rM if x_[0][0] != "MARK"]

                def merge2(rA, rB):
                    nA, nB = len(rA), len(rB)
                    iA = iB = 0
                    while iA < nA or iB < nB:
                        fA = iA / nA if nA else 1.0
                        fB = iB / nB if nB else 1.0
                        if iB < nB and (fB <= fA or iA >= nA):
                            for (a_, k_) in rB[iB:iB + BUR]:
                                P.add(*a_, **k_)
                            iB += BUR
                        else:
                            for (a_, k_) in rA[iA:iA + BUR]:
                                P.add(*a_, **k_)
                            iA += BUR

                if TAILF:
                    rT = record(emit_tail, *tl)
                    cut = next(i_ for i_, (a_, k_) in enumerate(rM)
                               if any(isinstance(w_, tuple) and w_[0] == "Hm" for w_ in k_.get("writes", ())))
                    merge2(rT, rM[:cut])
                    rM = rM[cut:]
                merge2(rF, rM)
                if not TAILF:
                    emit_tail(*tiles[g + 1])
            else:
                if TAILF:
                    emit_tail(*tl)
                emit_F(*tl)
                if g + 1 < len(tiles):
                    emit_M(*tiles[g + 1])
                    if not TAILF:
                        emit_tail(*tiles[g + 1])
        while P.bg:
            P._in_bg = True
            P.bg.pop(0)()
            P._in_bg = False
        if os.environ.get("KDEBUG"):
            print("P.ops:", len(P.ops))
        P.emit()
    return nc


def _consts():
    c = np.zeros((128, NCONST), np.float32)
    c[:, CC["ident"]:CC["ident"] + 128] = np.eye(128, dtype=np.float32)
    j = np.arange(128)
    c[:, CC["tri"]:CC["tri"] + 128] = (j[:, None] <= j[None, :]).astype(np.float32)
    nm = np.where(j[:, None] > j[None, :], -30000.0, 0.0).astype(np.float32)
    c[:, CC["negm"]:CC["negm"] + 128] = nm
    c[:, CC["sgn"]:CC["sgn"] + 8] = -1.0
    c[:, CC["sgn"] + 8:CC["sgn"] + 16] = 1.0
    c[:, CC["ones"]:CC["ones"] + 128] = 1.0
    return c


def _kmaj(w, nk):
    n = w.shape[1]
    return np.ascontiguousarray(w.reshape(nk, 128, n).transpose(1, 0, 2)).reshape(128, nk * n)


def _chunkvec(v):
    return np.ascontiguousarray(v.reshape(-1, 128).T)


def _prep(inp):
    wall = np.zeros((128, WPAD), np.float32)
    modw = np.zeros((L * 12, 128, 4096), np.float32)
    vec = np.zeros((L, 128, NV), np.float32)
    bv = np.zeros((L, 40), np.float32)
    for l in range(L):
        w_in = inp["w_in"][l]
        q, k, v = w_in[:, 0:512], w_in[:, 512:1024], w_in[:, 1024:1536]
        f = w_in[:, 1536:1544]
        z = w_in[:, 1544:2056]
        xbc = w_in[:, 2056:3080]
        dt = w_in[:, 3080:3088]
        fm = np.concatenate([q, k, z, xbc], axis=1)
        base = l * WL
        def put(key, arr):
            off, n = WB[key]
            assert arr.shape == (128, n), (key, arr.shape, n)
            wall[:, base + off:base + off + n] = arr
        for c in range(10):
            put(("INF", c), _kmaj(fm[:, c * 256:(c + 1) * 256], 8))
        for c in range(2):
            put(("INV", c), _kmaj(v[:, c * 256:(c + 1) * 256], 8))
        put(("INS",), _kmaj(np.concatenate([f, dt], axis=1), 8))
        wo = inp["w_out"][l]
        for c in range(4):
            put(("OUT", c), _kmaj(wo[:, c * 256:(c + 1) * 256], 8))
        up = inp["ffn_w_up"][l]
        for b in range(22):
            blk = np.concatenate([up[:, b * 128:(b + 1) * 128], up[:, DFF + b * 128:DFF + (b + 1) * 128]], axis=1)
            put(("UP", b), _kmaj(blk, 8))
        dn = inp["ffn_w_down"][l]
        for m in range(16):
            c_, kh = m // 2, m % 2
            put(("DOWN", m), _kmaj(dn[kh * 1408:(kh + 1) * 1408, c_ * 128:(c_ + 1) * 128], 11))
        mw = inp["mod_w"][l]
        for b in range(12):
            modw[l * 12 + b] = _kmaj(mw[:, b * 512:(b + 1) * 512], 8)
        vec[l, :, VC["modb"]:VC["modb"] + 48] = _chunkvec(inp["mod_b"][l])
        vec[l, :, VC["nmg"]:VC["nmg"] + 8] = _chunkvec(inp["norm_mix_g"][l])
        vec[l, :, VC["nfg"]:VC["nfg"] + 8] = _chunkvec(inp["norm_ffn_g"][l])
        vec[l, :, VC["ang"]:VC["ang"] + 4] = _chunkvec(inp["attn_norm_g"][l])
        vec[l, :, VC["sng"]:VC["sng"] + 4] = _chunkvec(inp["ssd_norm_g"][l])
        for kk in range(4):
            vec[l, :, VC["cw"] + kk * 8:VC["cw"] + kk * 8 + 8] = _chunkvec(inp["ssd_conv_w"][l][kk])
        vec[l, :, VC["cb"]:VC["cb"] + 8] = _chunkvec(inp["ssd_conv_b"][l])
        for kk in range(3):
            vec[l, :, VC["fcw"] + kk * 44:VC["fcw"] + kk * 44 + 44] = _chunkvec(inp["ffn_conv_w"][l][kk])
        vec[l, :, VC["fcb"]:VC["fcb"] + 44] = _chunkvec(inp["ffn_conv_b"][l])
        vec[l, :, VC["fin"]:VC["fin"] + 8] = _chunkvec(inp["final_g"])
        bv[l, 0:8] = inp["fox_forget_b"][l]
        bv[l, 8:16] = inp["ssd_dt_bias"][l]
        bv[l, 16:24] = inp["ssd_a_log"][l]
        bv[l, 24:32] = inp["ssd_d"][l]
    return wall, modw, vec, bv


def kernel(**inputs):
    inp = {k: np.asarray(v, dtype=np.float32) for k, v in inputs.items()}
    wall, modw, vec, bv = _prep(inp)
    consts = _consts()
    nc = build_nc()
    x = inp["x"]
    c = inp["c"]
    in_maps = []
    for b in range(8):
        in_maps.append({
            "x": np.ascontiguousarray(x[b]),
            "c": np.ascontiguousarray(c[b].reshape(8, 128).T),
            "wall": wall, "modw": modw, "vec": vec, "bv": bv, "consts": consts,
            "grow": np.ascontiguousarray(inp["attn_norm_g"]),
        })
    res = run_bass_kernel_spmd(nc, in_maps, core_ids=list(range(8)))
    return np.stack([np.asarray(r["out"], dtype=np.float32) for r in res.results], axis=0)
```

```python
import contextlib
import os
import numpy as np
import concourse.bass as bass
import concourse.mybir as mybir
from concourse.alu_op_type import AluOpType as ALU
from concourse.bass_utils import run_bass_kernel_spmd

F32 = mybir.dt.float32
BF16 = mybir.dt.bfloat16
AF = mybir.ActivationFunctionType

ENGS = ("tensor", "vector", "scalar", "gpsimd", "sync")

S = 4096
D = 1024
L = 2
T = 512
NT = S // T
DFF = 2816
EPS = 1e-6
PIECE = 2048
NEAR = int(os.environ.get("KNEAR", "3"))
NOPOOL = bool(int(os.environ.get("KNOPOOL", "1")))
BIGOK = bool(int(os.environ.get("KBIGOK", "1")))

WB = {}
_off = 0
def _wb(name, n):
    global _off
    WB[name] = (_off, n)
    _off += n
for _c in range(10):
    _wb(("INF", _c), 8 * 256)
for _c in range(2):
    _wb(("INV", _c), 8 * 256)
_wb(("INS",), 8 * 16)
for _c in range(4):
    _wb(("OUT", _c), 8 * 256)
for _b in range(22):
    _wb(("UP", _b), 8 * 256)
for _m in range(16):
    _wb(("DOWN", _m), 11 * 128)
WL = _off
WTOT = L * WL
NPIECE = (WTOT + PIECE - 1) // PIECE
WPAD = NPIECE * PIECE

VC = {}
_v = 0
for _n, _k in (("modb", 48), ("nmg", 8), ("nfg", 8), ("ang", 4), ("sng", 4), ("cw", 32), ("cb", 8),
               ("fcw", 132), ("fcb", 44), ("fin", 8)):
    VC[_n] = _v
    _v += _k
NV = _v
CC = {"ident": 0, "tri": 128, "negm": 256, "sgn": 384, "ones": 400}
NCONST = 528


class Op:
    __slots__ = ("eng", "fn", "id", "is_dma", "deps", "needs_inc", "tok", "dsem", "dtarget", "dprev", "sz")


class Prog:
    def __init__(self, nc, n_dma_sems=40):
        self.nc = nc
        self.ops = []
        self.last_w = {}
        self.readers = {}
        self.n_dma_sems = n_dma_sems

    def add(self, eng, fn, reads=(), writes=(), dma=False, sz=0):
        if eng == "MARK":
            return None
        bg = getattr(self, "bg", None)
        if bg and not getattr(self, "_in_bg", False):
            for k in reads:
                if isinstance(k, tuple) and k[0] == "wbf":
                    while bg and k not in self.last_w:
                        self._in_bg = True
                        if getattr(self, "bg_sp", None):
                            self.bg_sp.pop(0)
                        bg.pop(0)()
                        self._in_bg = False
        if bg and not getattr(self, "_in_bg", False):
            self._fg = getattr(self, "_fg", 0) + 1
            if self._fg % self.bg_every == 0:
                self._in_bg = True
                if getattr(self, "bg_sp", None):
                    self.bg_sp.pop(0)
                bg.pop(0)()
                self._in_bg = False
        op = Op()
        op.eng = eng
        op.fn = fn
        op.id = len(self.ops)
        op.is_dma = dma
        op.sz = sz
        op.needs_inc = False
        op.tok = 0
        deps = set()
        for k in reads:
            w = self.last_w.get(k)
            if w is not None:
                deps.add(w)
            if isinstance(k, tuple) and k[0] == "ps":
                for r in self.readers.get(k, ()):
                    if self.ops[r].eng != eng:
                        deps.add(r)
        for k in writes:
            w = self.last_w.get(k)
            if w is not None:
                deps.add(w)
            for r in self.readers.get(k, ()):
                deps.add(r)
        op.deps = deps
        for k in reads:
            self.readers.setdefault(k, []).append(op.id)
        for k in writes:
            self.last_w[k] = op.id
            self.readers[k] = []
        self.ops.append(op)
        return op.id

    def emit(self):
        nc = self.nc
        ops = self.ops
        for op in ops:
            for d in op.deps:
                dop = ops[d]
                if dop.is_dma:
                    continue
                if dop.eng != op.eng or op.is_dma or op.eng != "tensor":
                    dop.needs_inc = True
        cnt = {e: 0 for e in ENGS}
        dma_cnt = [0] * self.n_dma_sems
        ndma = 0
        for op in ops:
            if op.is_dma:
                k = ndma % self.n_dma_sems
                ndma += 1
                op.dsem = k
                op.dprev = dma_cnt[k]
                dma_cnt[k] += 16
                op.dtarget = dma_cnt[k]
            elif op.needs_inc:
                cnt[op.eng] += 1
                op.tok = cnt[op.eng]
        with contextlib.ExitStack() as st:
            esem = {e: st.enter_context(nc.semaphore("s_" + e)) for e in ENGS}
            dsem = [st.enter_context(nc.semaphore("d_%d" % i)) for i in range(self.n_dma_sems)]
            block = st.enter_context(nc.Block())
            per_eng = {e: [o for o in ops if o.eng == e] for e in ENGS}
            eidx = {}
            for e in ENGS:
                for n_, o in enumerate(per_eng[e]):
                    eidx[o.id] = n_
            final_dma = {}
            for o in ops:
                if o.is_dma:
                    final_dma[o.dsem] = o.dtarget
            n_dma_sems = self.n_dma_sems

            def make(ename):
                def body(eng):
                    waited_e = {e: 0 for e in ENGS}
                    waited_d = [0] * n_dma_sems
                    for op in per_eng[ename]:
                        need_e = {}
                        need_d = {}
                        for d in op.deps:
                            dop = ops[d]
                            if dop.is_dma:
                                if dop.dtarget > waited_d[dop.dsem]:
                                    need_d[dop.dsem] = max(need_d.get(dop.dsem, 0), dop.dtarget)
                            else:
                                if dop.eng == ename and not op.is_dma and (ename == "tensor" or eidx[op.id] - eidx[dop.id] > NEAR
                                                                           or (BIGOK and min(op.sz, dop.sz) >= 256)):
                                    continue
                                if dop.tok > waited_e[dop.eng]:
                                    need_e[dop.eng] = max(need_e.get(dop.eng, 0), dop.tok)
                        if op.is_dma and op.dprev > waited_d[op.dsem]:
                            need_d[op.dsem] = max(need_d.get(op.dsem, 0), op.dprev)
                        for e2, v in need_e.items():
                            eng.wait_ge(esem[e2], v)
                            waited_e[e2] = v
                        for k, v in need_d.items():
                            eng.wait_ge(dsem[k], v)
                            waited_d[k] = v
                        ins = op.fn(eng)
                        if op.is_dma:
                            ins.then_inc(dsem[op.dsem], 16)
                        elif op.needs_inc:
                            ins.then_inc(esem[ename], 1)
                    if ename == "sync":
                        for k, v in final_dma.items():
                            if v > waited_d[k]:
                                eng.wait_ge(dsem[k], v)
                return body

            for e in ENGS:
                if per_eng[e] or e == "sync":
                    getattr(block, e)(make(e))


def build_nc(n_layers=L, n_tiles=NT, dbg=None, merge=True):
    nc = bass.Bass("TRN2", target_bir_lowering=False)
    x_in = nc.dram_tensor("x", [S, D], F32, kind="ExternalInput").ap()
    c_in = nc.dram_tensor("c", [128, 8], F32, kind="ExternalInput").ap()
    w_in = nc.dram_tensor("wall", [128, WPAD], F32, kind="ExternalInput").ap()
    modw_in = nc.dram_tensor("modw", [L * 12, 128, 4096], F32, kind="ExternalInput").ap()
    vec_in = nc.dram_tensor("vec", [L, 128, NV], F32, kind="ExternalInput").ap()
    bv_in = nc.dram_tensor("bv", [L, 40], F32, kind="ExternalInput").ap()
    const_in = nc.dram_tensor("consts", [128, NCONST], F32, kind="ExternalInput").ap()
    grow_in = nc.dram_tensor("grow", [L, 512], F32, kind="ExternalInput").ap()
    out = nc.dram_tensor("out", [S, D], F32, kind="ExternalOutput").ap()
    wb_d = nc.dram_tensor("wbf", [128, WPAD], BF16, kind="Internal").ap()
    x0T = nc.dram_tensor("x0T", [8, 128, S], F32, kind="Internal").ap()
    x1T = nc.dram_tensor("x1T", [8, 128, S], F32, kind="Internal").ap()
    dbg_out = None
    if dbg:
        dbg_out = {k: nc.dram_tensor("dbg_" + k, shp, F32, kind="ExternalOutput").ap() for k, shp in dbg.items()}

    P = Prog(nc)
    with contextlib.ExitStack() as st:
        def sb(name, shape, dt):
            return st.enter_context(nc.sbuf_tensor(name, shape, dt))

        KT = sb("KT", [128, 4, S], BF16)
        V = sb("V", [128, 32, 8, 65], BF16)
        gbc = sb("gbc", [128, 512], F32)
        rsm = sb("rsm", [128, 8], F32)
        NRM = int(os.environ.get("KNRM", "2"))
        ringM = [sb("wrM%d" % i, [128, 2048], BF16) for i in range(NRM)]
        ringF = [sb("wrF%d" % i, [128, 2048], BF16) for i in range(int(os.environ.get("KNRF", "2")))]
        xT = sb("xT", [128, 8, 512], F32)
        Hm = sb("Hm", [128, 8, 512], BF16)
        Hf = sb("Hf", [128, 8, 512], BF16)
        sq = sb("sq", [128, 2, 512], BF16)
        rbc = sb("rbc", [128, 512], F32)
        recip = sb("recip", [128, 512], F32)
        rbcY = recip
        ytok = rbc
        qT = sb("qT", [128, 4, 512], BF16)
        zs = sb("zs", [128, 4, 512], BF16)
        U16 = sb("U16", [128, 8, 516], F32)
        xsTb = sb("xsTb", [128, 4, 512], BF16)
        BTb = sb("BTb", [128, 2, 512], BF16)
        CTb = sb("CTb", [128, 2, 512], BF16)
        xs_tok = sb("xs_tok", [128, 512], BF16)
        xdt = sb("xdt", [128, 512], BF16)
        xdte = sb("xdte", [128, 512], BF16)
        B_tok = sb("B_tok", [128, 256], BF16)
        PT = sb("PT", [128, 2, 2, 512], BF16)
        gT = sb("gT", [128, 22, 512], BF16)
        ubuf = sb("ubuf", [128, 2, 514], BF16)
        tmp = sb("tmp", [128, 2, 512], F32)
        MT = sb("MT", [128, 8, 128], BF16)
        consts = sb("consts_sb", [128, NCONST], F32)
        identb = sb("identb", [128, 128], BF16)
        trib = sb("trib", [128, 128], BF16)
        onesb = sb("onesb", [128, 128], BF16)
        vec = sb("vec_sb", [128, L, NV], F32)
        bvb = sb("bvb", [128, L, 40], F32)
        modv = sb("modv", [128, L, 48], F32)
        gs = sb("gs", [128, L, 16], F32)
        hgf = sb("hgf", [128, L, 8], F32)
        hcw = sb("hcw", [128, 40], F32)
        cact = sb("cact", [128, 8], F32)
        F_all = sb("F_all", [128, 32, 8], F32)
        biasT = sb("biasT", [128, 2, 32, 8], F32)
        fref = sb("fref", [128, 2, 8], F32)
        carryF = sb("carryF", [128, 8], F32)
        sm16 = sb("sm16", [128, 4, 16], F32)
        a16 = sb("a16", [128, 4, 16], F32)
        acs = sb("acs", [128, 4, 16], F32)
        tot16 = sb("tot16", [128, 4, 16], F32)
        small = sb("small", [128, 8, 8], F32)
        aneg = sb("aneg", [128, 8], F32)
        Dbc = sb("Dbc", [128, 8], F32)
        Sst = sb("Sst", [128, 2, 256], F32)
        Sbf = sb("Sbf", [128, 2, 256], BF16)
        uhalo = sb("uhalo", [128, 44, 2], F32)
        wS = sb("wS", [128, 8, 16], BF16)
        pb = [st.enter_context(nc.psum_tensor("pb%d" % i, [128, 512], F32)) for i in range(8)]
        if os.environ.get("KDEBUG"):
            print("SBUF bytes remaining per partition:", nc.sbuf_bytes_remaining)
        stg = [KT[:, i, 2048:4096].bitcast(F32) for i in range(4)]
        stgk = [[("KT", i, jt) for jt in range(4, 8)] for i in range(4)]
        stgb = [V[:, 16 + 2 * i:18 + 2 * i, :, :].rearrange("p a h d -> p (a h d)")[:, 0:1024] for i in range(4)]
        stgbk = [[("V", 16 + 2 * i), ("V", 17 + 2 * i)] for i in range(4)]

        ident = consts[:, CC["ident"]:CC["ident"] + 128]
        tri = consts[:, CC["tri"]:CC["tri"] + 128]
        negm = consts[:, CC["negm"]:CC["negm"] + 128]
        sgn = consts[:, CC["sgn"]:CC["sgn"] + 16]
        onesf = consts[:, CC["ones"]:CC["ones"] + 128]

        def dma(out_ap, in_ap, reads, writes, q="sync"):
            if os.environ.get("KQ"):
                q = os.environ["KQ"]
            P.add(q, lambda e: e.dma_start(out=out_ap, in_=in_ap), reads=reads, writes=writes, dma=True)

        def mm(o, lhsT, rhs, start, stop, reads, writes):
            P.add("tensor", lambda e: e.matmul(o, lhsT=lhsT, rhs=rhs, start=start, stop=stop), reads=reads, writes=writes)

        def tr(o, in_, idn, reads, writes):
            P.add("tensor", lambda e: e.transpose(o, in_, idn), reads=reads, writes=writes)

        def _sz(o):
            try:
                return int(o.free_size())
            except Exception:
                return 0

        def act(o, in_, func, reads, writes, bias=None, scale=None):
            kw = {}
            if bias is not None:
                kw["bias"] = bias
            if scale is not None:
                kw["scale"] = scale
            P.add("scalar", lambda e: e.activation(out=o, in_=in_, func=func, **kw), reads=reads, writes=writes, sz=_sz(o))

        def tt(o, a, b, op, reads, writes):
            P.add("vector", lambda e: e.tensor_tensor(out=o, in0=a, in1=b, op=op), reads=reads, writes=writes, sz=_sz(o))

        def ptt(o, a, b, op, reads, writes):
            P.add("vector" if NOPOOL else "gpsimd", lambda e: e.tensor_tensor(out=o, in0=a, in1=b, op=op), reads=reads, writes=writes, sz=_sz(o))

        def ts(o, a, s1, s2, op0, op1, reads, writes):
            if op1 is None:
                P.add("vector", lambda e: e.tensor_scalar(out=o, in0=a, scalar1=s1, scalar2=None, op0=op0), reads=reads, writes=writes, sz=_sz(o))
            else:
                P.add("vector", lambda e: e.tensor_scalar(out=o, in0=a, scalar1=s1, scalar2=s2, op0=op0, op1=op1), reads=reads, writes=writes, sz=_sz(o))

        def stt(o, a, s_, b, op0, op1, reads, writes):
            P.add("vector", lambda e: e.scalar_tensor_tensor(out=o, in0=a, scalar=s_, in1=b, op0=op0, op1=op1), reads=reads, writes=writes, sz=_sz(o))

        def vcopy(o, a, reads, writes):
            P.add("vector", lambda e: e.tensor_copy(out=o, in_=a), reads=reads, writes=writes, sz=_sz(o))

        def acopy(o, a, reads, writes):
            P.add("scalar", lambda e: e.copy(out=o, in_=a), reads=reads, writes=writes, sz=_sz(o))

        def vmemset(o, val, writes):
            P.add("vector", lambda e: e.memset(o, val), writes=writes)

        def dbg_dump(name, src_ap, rkeys):
            if dbg_out is not None and name in dbg_out:
                dma(dbg_out[name], src_ap, rkeys, [("dbg", name)], q="gpsimd")

        HmK = [("Hm", c) for c in range(8)]
        HfK = [("Hf", c) for c in range(8)]
        U16K = [("U16", c) for c in range(8)]
        xTK = [("xT", c) for c in range(8)]

        dma(consts[:], const_in, [], ["consts"])
        for l in range(L):
            dma(vec[:, l, :], vec_in[l], [], [("vec", l)])
            dma(bvb[:, l, :], bv_in[l].partition_broadcast(128), [], [("bv", l)])
        dma(cact[:], c_in, [], ["cact"])
        vcopy(identb[:], ident, ["consts"], ["identb"])
        vcopy(trib[:], tri, ["consts"], ["trib"])
        vcopy(onesb[:], onesf, ["consts"], ["onesb"])
        act(cact[:], cact[:], AF.Silu, ["cact"], ["cact"])

        NSPC = WPAD // 1024
        cvn = {"n": 0}
        converted = set()

        def conv_subpiece(sp, q_in="sync", q_out="sync"):
            converted.add(sp)
            i_ = cvn["n"] % 4
            cvn["n"] += 1
            dma(stg[i_], w_in[:, sp * 1024:(sp + 1) * 1024], [], stgk[i_], q=q_in)
            if cvn["n"] % 2 == 0:
                vcopy(stgb[i_], stg[i_], stgk[i_], stgbk[i_])
            else:
                acopy(stgb[i_], stg[i_], stgk[i_], stgbk[i_])
            dma(wb_d[:, sp * 1024:(sp + 1) * 1024], stgb[i_], stgbk[i_], [("wbf", sp)], q=q_out)

        n_early = (WB[("UP", 0)][0] + 1023) // 1024
        for sp in range(n_early):
            conv_subpiece(sp)
        rowbuf = recip[0:1, :]
        vmemset(V[:, 0:16, :, 64:65], 1.0, [("V", q) for q in range(16)])
        vmemset(V[:, 24:32, :, 64:65], 1.0, [("V", q) for q in range(24, 32)])

        def mod_block(l, blk, bank, q):
            dma(U16[:, :, 0:512], modw_in[l * 12 + blk].rearrange("p (k n) -> p k n", k=8), [], U16K, q=q)
            for k in range(8):
                mm(pb[bank][0:1, :], cact[:, k:k + 1], U16[:, k, 0:512], k == 0, k == 7, [("U16", k), "cact"], [("ps", bank)])
            vcopy(rowbuf[:], pb[bank][0:1, :], [("ps", bank)], ["recip"])
            for jj in range(4):
                mm(pb[bank][:, jj:jj + 1], rowbuf[0:1, jj * 128:(jj + 1) * 128], onesf[0:1, 0:1], True, True,
                   ["recip", "consts"], [("ps", bank)])
            tt(modv[:, l, blk * 4:blk * 4 + 4], pb[bank][:, 0:4], vec[:, l, VC["modb"] + blk * 4:VC["modb"] + blk * 4 + 4], ALU.add,
               [("ps", bank), ("vec", l)], [("modv", l)])

        def mod_finish(l):
            stt(gs[:, l, 0:8], modv[:, l, 8:16], 1.0, vec[:, l, VC["nmg"]:VC["nmg"] + 8], ALU.add, ALU.mult,
                [("modv", l), ("vec", l)], [("gs", l)])
            stt(gs[:, l, 8:16], modv[:, l, 32:40], 1.0, vec[:, l, VC["nfg"]:VC["nfg"] + 8], ALU.add, ALU.mult,
                [("modv", l), ("vec", l)], [("gs", l)])
            ts(hgf[:, l, :], modv[:, l, 40:48], 0.5, None, ALU.mult, None, [("modv", l)], [("gs", l)])

        for blk in range(12):
            mod_block(0, blk, 5, "sync")
        mod_finish(0)
        defer_mod = [(1, blk) for blk in range(12)] if n_layers > 1 else []

        wst = {"M": 0, "F": 0}
        bg_sp = []

        def need_conv(sp_lo, sp_hi):
            bgl = None
            while bgl and any(sp_ not in converted for sp_ in range(sp_lo, sp_hi + 1)):
                if bg_sp:
                    bg_sp.pop(0)
                P._in_bg = True
                bgl.pop(0)()
                P._in_bg = False

        def wload(stream, l, key):
            ring = ringM if stream == "M" else ringF
            off, n = WB[key]
            slot = wst[stream] % len(ring)
            wst[stream] += 1
            a_ = l * WL + off
            need_conv(a_ // 1024, (a_ + n - 1) // 1024)
            dma(ring[slot][:, 0:n], wb_d[:, a_:a_ + n], [("wbf", i) for i in range(a_ // 1024, (a_ + n - 1) // 1024 + 1)],
                [("wr" + stream, slot)], q=("gpsimd" if (stream == "F" and os.environ.get("KFQ")) else "sync"))
            return ring[slot], ("wr" + stream, slot)

        sqn = {"n": 0}

        def sq_acc(src, skey, bank, first, last, side):
            i = sqn["n"]
            sqn["n"] += 1
            if side == "M":
                dst, dk = ((xdt, "xdt"), (xdte, "xdte"))[i % 2]
                dst = dst[:]
            else:
                dst, dk = sq[:, i % 2, :], ("sq", i % 2)
            if i % 3 == 2:
                ptt(dst, src, src, ALU.mult, [skey], [dk])
            else:
                tt(dst, src, src, ALU.mult, [skey], [dk])
            mm(pb[bank][:], onesb[:], dst, first, last, [dk, "onesb"], [("ps", bank)])

        def rstd_from(dst, bank, nfeat, wkey, eps=EPS):
            act(dst, pb[bank][:], AF.Ln, [("ps", bank)], [wkey], bias=eps, scale=1.0 / nfeat)
            act(dst, dst, AF.Exp, [wkey], [wkey], scale=-0.5)

        nbM = {"n": 0}

        def mbank():
            b = nbM["n"] % 4
            nbM["n"] += 1
            return b

        def emit_M(l, j):
            t0 = j * T
            if j == 0:
                a_ = l * WL + WB[("INS",)][0]
                need_conv(a_ // 1024, (a_ + 127) // 1024)
                dma(wS[:], wb_d[:, a_:a_ + 128].rearrange("p (k n) -> p k n", k=8),
                    [("wbf", i) for i in range(a_ // 1024, (a_ + 127) // 1024 + 1)], ["wS"])
                dma(gbc[:], grow_in[l].partition_broadcast(128), [], ["gbc"], q="gpsimd")
                ts(hcw[:], vec[:, l, VC["cw"]:VC["cw"] + 40], 0.5, None, ALU.mult, None, [("vec", l)], ["hcw"])
                act(aneg[:], bvb[:, l, 16:24], AF.Exp, [("bv", l)], ["aneg"])
                ts(aneg[:], aneg[:], -1.0, None, ALU.mult, None, ["aneg"], ["aneg"])
                vcopy(Dbc[:], bvb[:, l, 24:32], [("bv", l)], ["Dbc"])
                vmemset(carryF[:], 0.0, ["carryF"])
                vmemset(Sst[:], 0.0, ["Sst"])
                vmemset(Sbf[:], 0.0, ["Sbf"])
            if defer_mod and ((l == 0 and j >= 1) or l == 1):
                for _ in range(2 if l == 0 else 12):
                    if defer_mod:
                        mod_block(*defer_mod.pop(0), 5, "gpsimd")
                if not defer_mod:
                    mod_finish(1)
            if l == 0:
                for blk in range(4):
                    if blk % 2 == 0:
                        xtok = PT[:].rearrange("p a b c -> p (a b c)").bitcast(F32)
                        hkeys = [("PT", 0, 0), ("PT", 0, 1), ("PT", 1, 0), ("PT", 1, 1)]
                    else:
                        xtok = qT[:].rearrange("p a b -> p (a b)").bitcast(F32)
                        hkeys = [("qT", m_) for m_ in range(4)]
                    dma(xtok, x_in[t0 + blk * 128:t0 + (blk + 1) * 128, :], [], hkeys, q="gpsimd")
                    for half in range(2):
                        bank = mbank()
                        for q in range(4):
                            c = half * 4 + q
                            tr(pb[bank][:, q * 128:(q + 1) * 128], xtok[:, c * 128:(c + 1) * 128], ident,
                               hkeys + ["consts"], [("ps", bank)])
                        vcopy(U16[:, half * 4:half * 4 + 4, blk * 128:(blk + 1) * 128],
                              pb[bank][:].rearrange("p (q t) -> p q t", q=4), [("ps", bank)],
                              [("U16", half * 4 + q) for q in range(4)])
                dma(x0T[:, :, t0:t0 + T].rearrange("c p t -> p c t"), U16[:, :, 0:512], U16K, [("x0T", j)], q="gpsimd")
            else:
                dma(U16[:, :, 0:512], x1T[:, :, t0:t0 + T].rearrange("c p t -> p c t"), [("x1T", j)], U16K, q="gpsimd")
            for c in range(8):
                sq_acc(U16[:, c, 0:512], ("U16", c), 4, c == 0, c == 7, "M")
            rstd_from(rbc[:], 4, D, "rbc")
            for c in range(8):
                bank = mbank()
                g_ = gs[:, l, c:c + 1]
                sh_ = modv[:, l, c:c + 1]
                tt(pb[bank][:], U16[:, c, 0:512], rbc[:], ALU.mult, [("U16", c), "rbc"], [("ps", bank)])
                if c % 2 == 0:
                    act(Hm[:, c, :], pb[bank][:], AF.Identity, [("ps", bank), ("gs", l), ("modv", l)], [("Hm", c)], bias=sh_, scale=g_)
                else:
                    ts(Hm[:, c, :], pb[bank][:], g_, sh_, ALU.mult, ALU.add, [("ps", bank), ("gs", l), ("modv", l)], [("Hm", c)])
            for c in range(8):
                if j == 0:
                    vmemset(U16[:, c, 0:3], 0.0, [("U16", c)])
                else:
                    vcopy(U16[:, c, 0:3], U16[:, c, 512:515], [("U16", c)], [("U16", c)])
            for cb in range(10):
                ring, rk = wload("M", l, ("INF", cb))
                w3 = ring[:, 0:2048].rearrange("p (k n) -> p k n", k=8)
                for m2 in range(2):
                    bank = mbank()
                    m = (cb % 2) * 2 + m2
                    for k in range(8):
                        mm(pb[bank][:], w3[:, k, m2 * 128:(m2 + 1) * 128], Hm[:, k, :], k == 0, k == 7,
                           [rk, ("Hm", k)], [("ps", bank)])
                    grp = cb // 2
                    if grp == 0:
                        acopy(qT[:, m, :], pb[bank][:], [("ps", bank)], [("qT", m)])
                    elif grp == 1:
                        vcopy(KT[:, m, t0:t0 + T], pb[bank][:], [("ps", bank)], [("KT", m, j)])
                    elif grp == 2:
                        act(zs[:, m, :], pb[bank][:], AF.Tanh, [("ps", bank)], [("zs", m)], scale=0.5)
                        stt(zs[:, m, :], zs[:, m, :], 1.0, pb[bank][:], ALU.add, ALU.mult, [("zs", m), ("ps", bank)], [("zs", m)])
                    else:
                        c = (grp - 3) * 4 + m
                        if m % 2 == 0:
                            vcopy(U16[:, c, 3:515], pb[bank][:], [("ps", bank)], [("U16", c)])
                        else:
                            acopy(U16[:, c, 3:515], pb[bank][:], [("ps", bank)], [("U16", c)])
            vr = []
            for cb in range(2):
                vr.append(wload("M", l, ("INV", cb)))
            for blk in range(4):
                bank = mbank()
                for cb in range(2):
                    ring, rk = vr[cb]
                    w3 = ring[:, 0:2048].rearrange("p (k n) -> p k n", k=8)
                    for k in range(8):
                        mm(pb[bank][:, cb * 256:(cb + 1) * 256], Hm[:, k, blk * 128:(blk + 1) * 128], w3[:, k, :], k == 0, k == 7,
                           [rk, ("Hm", k)], [("ps", bank)])
                if blk % 2 == 0:
                    vcopy(V[:, 4 * j + blk, :, 0:64], pb[bank][:].rearrange("p (h d) -> p h d", h=8), [("ps", bank)], [("V", 4 * j + blk)])
                else:
                    acopy(V[:, 4 * j + blk, :, 0:64], pb[bank][:].rearrange("p (h d) -> p h d", h=8), [("ps", bank)], [("V", 4 * j + blk)])
            for blk in range(4):
                for k in range(8):
                    mm(pb[5][:, blk * 16:(blk + 1) * 16], Hm[:, k, blk * 128:(blk + 1) * 128], wS[:, k, :], k == 0, k == 7,
                       [("Hm", k), "wS"], [("ps", 5)])
            ps16 = pb[5][:, 0:64].rearrange("p (b n) -> p b n", b=4)
            tt(sm16[:], ps16, bvb[:, l, 0:16].unsqueeze(1).broadcast_to([128, 4, 16]), ALU.add, [("ps", 5), ("bv", l)], ["sm16"])
            tt(sm16[:], sm16[:], sgn.unsqueeze(1).broadcast_to([128, 4, 16]), ALU.mult, ["sm16", "consts"], ["sm16"])
            act(sm16[:], sm16[:], AF.Exp, ["sm16"], ["sm16"])
            act(sm16[:], sm16[:], AF.Ln, ["sm16"], ["sm16"], bias=1.0)
            vcopy(a16[:, :, 0:8], sm16[:, :, 0:8], ["sm16"], ["a16"])
            tt(a16[:, :, 8:16], sm16[:, :, 8:16], aneg[:].unsqueeze(1).broadcast_to([128, 4, 8]), ALU.mult, ["sm16", "aneg"], ["a16"])
            for blk in range(4):
                mm(pb[4][:, blk * 32:blk * 32 + 16], tri, a16[:, blk, :], True, True, ["consts", "a16"], [("ps", 4)])
                mm(pb[4][:, blk * 32 + 16:blk * 32 + 32], onesf, a16[:, blk, :], True, True, ["consts", "a16"], [("ps", 4)])
            cs4 = pb[4][:, 0:128].rearrange("p (b n) -> p b n", b=4)
            vcopy(acs[:], cs4[:, :, 0:16], [("ps", 4)], ["acs"])
            vcopy(tot16[:], cs4[:, :, 16:32], [("ps", 4)], ["tot16"])
            for blk in range(4):
                gb = 4 * j + blk
                tt(F_all[:, gb, :], acs[:, blk, 0:8], carryF[:], ALU.add, ["acs", "carryF"], [("F_all", gb)])
                tt(carryF[:], carryF[:], tot16[:, blk, 0:8], ALU.add, ["carryF", "tot16"], ["carryF"])
                if blk == 0:
                    vcopy(fref[:, 0, :], carryF[:], ["carryF"], ["fref"])
                if blk == 2:
                    vcopy(fref[:, 1, :], carryF[:], ["carryF"], ["fref"])
            nkb = 4 * j + 4
            for half in range(2):
                tt(biasT[:, half, 0:nkb, :], F_all[:, 0:nkb, :], fref[:, half, :].unsqueeze(1).broadcast_to([128, nkb, 8]),
                   ALU.subtract, [("F_all", g) for g in range(nkb)] + ["fref"], ["biasT"])

            P.add("MARK", None)
            rec = []
            _radd = P.add
            P.add = lambda *a_, **k_: rec.append((a_, k_))
            X, Y = 4, 5
            for c in range(8):
                cwv = lambda k: hcw[:, k * 8 + c:k * 8 + c + 1]
                bk = X + c % 2
                acc = pb[bk][:]
                ts(acc, U16[:, c, 0:512], cwv(0), hcw[:, 32 + c:33 + c], ALU.mult, ALU.add,
                   [("U16", c), "hcw"], [("ps", bk)])
                for k in range(1, 4):
                    stt(acc, U16[:, c, k:k + 512], cwv(k), acc, ALU.mult, ALU.add, [("U16", c), "hcw", ("ps", bk)], [("ps", bk)])
                if c < 4:
                    dst_, dk_ = xsTb[:, c, :], ("xsTb", c)
                elif c < 6:
                    dst_, dk_ = BTb[:, c - 4, :], ("BTb", c - 4)
                else:
                    dst_, dk_ = CTb[:, c - 6, :], ("CTb", c - 6)
                act(dst_, acc, AF.Tanh, [("ps", bk)], [dk_])
                stt(dst_, dst_, 1.0, acc, ALU.add, ALU.mult, [dk_, ("ps", bk)], [dk_])
            sm = small
            for blk in range(4):
                tk = slice(blk * 128, (blk + 1) * 128)
                for c in range(4):
                    mm(pb[X][:, c * 128:(c + 1) * 128], xsTb[:, c, tk], identb[:], True, True, [("xsTb", c), "identb"], [("ps", X)])
                for g in range(2):
                    mm(pb[Y][:, g * 128:(g + 1) * 128], BTb[:, g, tk], identb[:], True, True, [("BTb", g), "identb"], [("ps", Y)])
                for g in range(2):
                    mm(pb[Y][:, 256 + g * 128:256 + (g + 1) * 128], BTb[:, g, tk], CTb[:, g, tk], True, True,
                       [("BTb", g), ("CTb", g)], [("ps", Y)])
                tt(sm[:, 0, :], tot16[:, blk, 8:16], acs[:, blk, 8:16], ALU.subtract, ["tot16", "acs"], ["small"])
                act(sm[:, 0, :], sm[:, 0, :], AF.Exp, ["small"], ["small"])
                act(sm[:, 1, :], acs[:, blk, 8:16], AF.Exp, ["acs", "small"], ["small"])
                act(sm[:, 2, :], tot16[:, blk, 8:16], AF.Exp, ["tot16", "small"], ["small"])
                tt(sm[:, 3, :], sm[:, 0, :], sm16[:, blk, 8:16], ALU.mult, ["small", "sm16"], ["small"])
                ts(sm[:, 4, :], acs[:, blk, 8:16], -1.0, None, ALU.mult, None, ["acs", "small"], ["small"])
                ps3 = pb[X][:].rearrange("p (h d) -> p h d", h=8)
                vcopy(xs_tok[:], pb[X][:], [("ps", X)], ["xs_tok"])
                tt(xdt[:].rearrange("p (h d) -> p h d", h=8), ps3, sm16[:, blk, 8:16].unsqueeze(2).broadcast_to([128, 8, 64]), ALU.mult,
                   [("ps", X), "sm16"], ["xdt"])
                tt(xdte[:].rearrange("p (h d) -> p h d", h=8), ps3, sm[:, 3, :].unsqueeze(2).broadcast_to([128, 8, 64]), ALU.mult,
                   [("ps", X), "small"], ["xdte"])
                acopy(B_tok[:], pb[Y][:, 0:256], [("ps", Y)], ["B_tok"])
                for hh in range(2):
                    for hq in range(4):
                        h = hh * 4 + hq
                        o_ = pb[X][:, hq * 128:(hq + 1) * 128]
                        mm(o_, acs[:, blk, 8 + h:9 + h].broadcast_to([128, 128]), ident, True, False, ["acs", "consts"], [("ps", X)])
                        mm(o_, ident, negm, False, True, ["consts"], [("ps", X)])
                    for hq in range(4):
                        h = hh * 4 + hq
                        act(MT[:, h, :], pb[X][:, hq * 128:(hq + 1) * 128], AF.Exp, [("ps", X), "small"], [("MT", h // 4)],
                            bias=sm[:, 4, h:h + 1])
                for g in range(2):
                    tt(MT[:, 4 * g:4 * g + 4, :], MT[:, 4 * g:4 * g + 4, :],
                       pb[Y][:, 256 + g * 128:256 + (g + 1) * 128].unsqueeze(1).broadcast_to([128, 4, 128]), ALU.mult,
                       [("MT", g), ("ps", Y)], [("MT", g)])
                for h in range(8):
                    mm(pb[X][:, h * 64:(h + 1) * 64], MT[:, h, :], xdt[:, h * 64:(h + 1) * 64], True, True,
                       [("MT", h // 4), "xdt"], [("ps", X)])
                for g in range(2):
                    mm(pb[Y][:, g * 256:(g + 1) * 256], CTb[:, g, tk], Sbf[:, g, :], True, True, [("CTb", g), "Sbf"], [("ps", Y)])
                yt3 = ytok[:].rearrange("p (h d) -> p h d", h=8)
                tt(yt3, pb[Y][:].rearrange("p (h d) -> p h d", h=8), sm[:, 1, :].unsqueeze(2).broadcast_to([128, 8, 64]), ALU.mult,
                   [("ps", Y), "small"], ["rbc"])
                tt(ytok[:], ytok[:], pb[X][:], ALU.add, ["rbc", ("ps", X)], ["rbc"])
                tt(pb[Y][:].rearrange("p (h d) -> p h d", h=8), xs_tok[:].rearrange("p (h d) -> p h d", h=8),
                   Dbc[:].unsqueeze(2).broadcast_to([128, 8, 64]), ALU.mult, ["xs_tok", "Dbc", ("ps", Y)], [("ps", Y)])
                tt(ytok[:], ytok[:], pb[Y][:], ALU.add, ["rbc", ("ps", Y)], ["rbc"])
                for g in range(2):
                    mm(pb[Y][:, g * 256:(g + 1) * 256], B_tok[:, g * 128:(g + 1) * 128], xdte[:, g * 256:(g + 1) * 256], True, True,
                       ["B_tok", "xdte"], [("ps", Y)])
                for g in range(2):
                    s3 = Sst[:, g, :].rearrange("p (r d) -> p r d", r=4)
                    tt(s3, s3, sm[:, 2, 4 * g:4 * g + 4].unsqueeze(2).broadcast_to([128, 4, 64]), ALU.mult, ["Sst", "small"], ["Sst"])
                    tt(Sst[:, g, :], Sst[:, g, :], pb[Y][:, g * 256:(g + 1) * 256], ALU.add, ["Sst", ("ps", Y)], ["Sst"])
                vcopy(Sbf[:], Sst[:], ["Sst"], ["Sbf"])
                for c in range(4):
                    tr(pb[X][:, c * 128:(c + 1) * 128], ytok[:, c * 128:(c + 1) * 128], ident, ["rbc", "consts"], [("ps", X)])
                for c in range(4):
                    tt(U16[:, 4 + c, blk * 128:(blk + 1) * 128], pb[X][:, c * 128:(c + 1) * 128], zs[:, c, tk], ALU.mult,
                       [("ps", X), ("zs", c)], [("U16", 4 + c)])
            P.add = _radd

            steps = [(hp, kb) for hp in range(4) for kb in range(nkb)]

            def qk(i):
                hp, kb = steps[i]
                m = kb - 4 * j
                c0 = 128 * m if m > 0 else 0
                ks = slice(kb * 128, (kb + 1) * 128)
                mm(pb[0][:, c0:512], KT[0:64, hp, ks], qT[0:64, hp, c0:512], True, True, [("KT", hp, kb // 4), ("qT", hp)], [("ps", 0)])
                mm(pb[1][:, c0:512], KT[64:128, hp, ks], qT[64:128, hp, c0:512], True, True, [("KT", hp, kb // 4), ("qT", hp)], [("ps", 1)])

            def softmax(i):
                hp, kb = steps[i]
                m = kb - 4 * j
                c0 = 128 * m if m > 0 else 0
                pbuf = i % 2
                for (hd, bk, xi) in ((2 * hp, 0, 0), (2 * hp + 1, 1, 1)):
                    for half in range(2):
                        lo = max(c0, 256 * half)
                        hi = 256 * (half + 1)
                        if lo >= hi:
                            continue
                        act(PT[:, pbuf, xi, lo:hi], pb[bk][:, lo:hi], AF.Exp, [("ps", bk), "biasT"],
                            [("PT", pbuf, xi)], bias=biasT[:, half, kb, hd:hd + 1], scale=0.125)
                    if m >= 0:
                        tt(PT[:, pbuf, xi, 128 * m:128 * m + 128], PT[:, pbuf, xi, 128 * m:128 * m + 128], trib[:], ALU.mult,
                           [("PT", pbuf, xi), "trib"], [("PT", pbuf, xi)])

            def pv(i):
                hp, kb = steps[i]
                m = kb - 4 * j
                pbuf = i % 2
                for xi in range(2):
                    h = 2 * hp + xi
                    bank = 2 + xi
                    for tc in range(4):
                        if m > tc:
                            continue
                        first = (kb == 0 and tc == 0)
                        P.add("tensor", lambda e, bank=bank, tc=tc, xi=xi, h=h, first=first, kb=kb, pbuf=pbuf: e.matmul(
                            pb[bank][:, tc * 65:(tc + 1) * 65], lhsT=PT[:, pbuf, xi, tc * 128:(tc + 1) * 128], rhs=V[:, kb, h, :],
                            start=first, stop=(kb == 4 * j + tc), skip_group_check=True),
                            reads=[("V", kb), ("PT", pbuf, xi)], writes=[("ps", bank)])
                if kb == nkb - 1:
                    for xi in range(2):
                        h = 2 * hp + xi
                        bank = 2 + xi
                        o3 = pb[bank][:, 0:260].rearrange("p (t d) -> p t d", t=4)
                        P.add("vector", lambda e, o3=o3, xi=xi: e.reciprocal(out=rsm[:, 4 * xi:4 * xi + 4].unsqueeze(2), in_=o3[:, :, 64:65]),
                              reads=[("ps", bank)], writes=["rsm"])
                        tt(U16[:, 0:4, h * 64:(h + 1) * 64], o3[:, :, 0:64], rsm[:, 4 * xi:4 * xi + 4].unsqueeze(2).broadcast_to([128, 4, 64]),
                           ALU.mult, [("ps", bank), "rsm"], [("U16", c) for c in range(4)])

            per_step = (len(rec) + len(steps) - 1) // len(steps)
            ri = 0
            qk(0)
            for i in range(len(steps)):
                softmax(i)
                if i + 1 < len(steps):
                    qk(i + 1)
                pv(i)
                for (a_, k_) in rec[ri:ri + per_step]:
                    P.add(*a_, **k_)
                ri += per_step
            for (a_, k_) in rec[ri:]:
                P.add(*a_, **k_)
            if l == 0 and j == 0:
                dbg_dump("ymix", U16[:, :, 0:512], U16K)

            ssq = small[:, 5, 0:4]
            for tc in range(4):
                bk = mbank()
                P.add("scalar", lambda e, tc=tc, bk=bk: e.activation(
                    out=pb[bk][:], in_=U16[:, tc, 0:512], func=AF.Square, accum_out=small[:, 5, tc:tc + 1]),
                    reads=[("U16", tc)], writes=[("ps", bk), "small"])
            act(ssq, ssq, AF.Ln, ["small"], ["small"], bias=EPS, scale=1.0 / 512)
            act(ssq, ssq, AF.Exp, ["small"], ["small"], scale=-0.5)
            for tc in range(4):
                stt(U16[:, tc, 0:512], U16[:, tc, 0:512], small[:, 5, tc:tc + 1], gbc[:], ALU.mult, ALU.mult,
                    [("U16", tc), "small", "gbc"], [("U16", tc)])
            for tc in range(4):
                bk = mbank()
                for c in range(4):
                    tr(pb[bk][:, c * 128:(c + 1) * 128], U16[:, tc, c * 128:(c + 1) * 128], ident, [("U16", tc), "consts"], [("ps", bk)])
                if tc % 2 == 0:
                    vcopy(Hm[:, 0:4, tc * 128:(tc + 1) * 128], pb[bk][:].rearrange("p (c t) -> p c t", c=4), [("ps", bk)],
                          [("Hm", c) for c in range(4)])
                else:
                    acopy(Hm[:, 0:4, tc * 128:(tc + 1) * 128], pb[bk][:].rearrange("p (c t) -> p c t", c=4), [("ps", bk)],
                          [("Hm", c) for c in range(4)])
            for (c_lo, nch, gname) in ((4, 2, "sng"), (6, 2, "sng")):
                for i in range(nch):
                    sq_acc(U16[:, c_lo + i, 0:512], ("U16", c_lo + i), 4, i == 0, i == nch - 1, "M")
                rstd_from(rbcY[:], 4, 128 * nch, "recip", eps=4.0 * EPS)
                for i in range(nch):
                    c = c_lo + i
                    gi = c - 4
                    stt(Hm[:, c, :], U16[:, c, 0:512], vec[:, l, VC[gname] + gi:VC[gname] + gi + 1], rbcY[:], ALU.mult, ALU.mult,
                        [("U16", c), ("vec", l), "recip"], [("Hm", c)])

        TAILF = bool(int(os.environ.get("KTAILF", "1")))

        def emit_tail(l, j):
            t0 = j * T
            src = x0T if l == 0 else x1T
            dma(xT[:], src[:, :, t0:t0 + T].rearrange("c p t -> p c t"), [("x0T" if l == 0 else "x1T", j)], xTK, q="gpsimd")
            for cb in range(4):
                ring, rk = wload("M", l, ("OUT", cb))
                w3 = ring[:, 0:2048].rearrange("p (k n) -> p k n", k=8)
                for m2 in range(2):
                    c = cb * 2 + m2
                    bank = (6 + c % 2) if TAILF else mbank()
                    for k in range(8):
                        mm(pb[bank][:], w3[:, k, m2 * 128:(m2 + 1) * 128], Hm[:, k, :], k == 0, k == 7, [rk, ("Hm", k)], [("ps", bank)])
                    stt(xT[:, c, :], pb[bank][:], modv[:, l, 16 + c:17 + c], xT[:, c, :], ALU.mult, ALU.add,
                        [("ps", bank), ("modv", l), ("xT", c)], [("xT", c)])
                    if not TAILF:
                        sq_acc(xT[:, c, :], ("xT", c), 6, c == 0, c == 7, "F")
            if TAILF:
                for c in range(8):
                    sq_acc(xT[:, c, :], ("xT", c), 6, c == 0, c == 7, "F")
            if l == 0 and j == 0:
                dbg_dump("xmid", xT[:], xTK)

        def emit_F(l, j):
            t0 = j * T
            if j == 0:
                vmemset(uhalo[:], 0.0, ["uhalo"])
            act(pb[7][:], pb[6][:], AF.Ln, [("ps", 6)], [("ps", 7)], bias=EPS, scale=1.0 / D)
            act(pb[7][:], pb[7][:], AF.Exp, [("ps", 7)], [("ps", 7)], scale=-0.5)
            for c in range(8):
                tb = tmp[:, c % 2, :]
                tk_ = ("tmp", c % 2)
                g_ = gs[:, l, 8 + c:9 + c]
                sh_ = modv[:, l, 24 + c:25 + c]
                tt(tb, xT[:, c, :], pb[7][:], ALU.mult, [("xT", c), ("ps", 7)], [tk_])
                if c % 2 == 0:
                    act(Hf[:, c, :], tb, AF.Identity, [tk_, ("gs", l), ("modv", l)], [("Hf", c)], bias=sh_, scale=g_)
                else:
                    ts(Hf[:, c, :], tb, g_, sh_, ALU.mult, ALU.add, [tk_, ("gs", l), ("modv", l)], [("Hf", c)])
            for gi in range(22):
                ring, rk = wload("F", l, ("UP", gi))
                w3 = ring[:, 0:2048].rearrange("p (k n) -> p k n", k=8)
                for u in range(2):
                    bank = 6 + u
                    for k in range(8):
                        mm(pb[bank][:], w3[:, k, u * 128:(u + 1) * 128], Hf[:, k, :], k == 0, k == 7, [rk, ("Hf", k)], [("ps", bank)])
                for (u, idx) in ((0, gi), (1, 22 + gi)):
                    bank = 6 + u
                    fw = lambda k, idx=idx: vec[:, l, VC["fcw"] + k * 44 + idx:VC["fcw"] + k * 44 + idx + 1]
                    kf = os.environ.get("KF_ACT", "0")
                    if kf == "1":
                        acopy(ubuf[:, u, 2:514], pb[bank][:], [("ps", bank)], [("ubuf", u)])
                        act(tmp[:, u, :], pb[bank][:], AF.Identity, [("ps", bank), ("vec", l)], [("tmp", u)],
                            bias=vec[:, l, VC["fcb"] + idx:VC["fcb"] + idx + 1], scale=fw(2))
                    elif kf == "2":
                        acopy(ubuf[:, u, 2:514], pb[bank][:], [("ps", bank)], [("ubuf", u)])
                        ts(tmp[:, u, :], pb[bank][:], fw(2), vec[:, l, VC["fcb"] + idx:VC["fcb"] + idx + 1], ALU.mult, ALU.add,
                           [("ps", bank), ("vec", l)], [("tmp", u)])
                    elif kf == "4" and u == 0:
                        acopy(ubuf[:, u, 2:514], pb[bank][:], [("ps", bank)], [("ubuf", u)])
                        act(tmp[:, u, :], pb[bank][:], AF.Identity, [("ps", bank), ("vec", l)], [("tmp", u)],
                            bias=vec[:, l, VC["fcb"] + idx:VC["fcb"] + idx + 1], scale=fw(2))
                    elif kf == "3":
                        vcopy(ubuf[:, u, 2:514], pb[bank][:], [("ps", bank)], [("ubuf", u)])
                        act(tmp[:, u, :], pb[bank][:], AF.Identity, [("ps", bank), ("vec", l)], [("tmp", u)],
                            bias=vec[:, l, VC["fcb"] + idx:VC["fcb"] + idx + 1], scale=fw(2))
                    else:
                        vcopy(ubuf[:, u, 2:514], pb[bank][:], [("ps", bank)], [("ubuf", u)])
                        ts(tmp[:, u, :], pb[bank][:], fw(2), vec[:, l, VC["fcb"] + idx:VC["fcb"] + idx + 1], ALU.mult, ALU.add,
                           [("ps", bank), ("vec", l)], [("tmp", u)])
                    vcopy(ubuf[:, u, 0:2], uhalo[:, idx, :], ["uhalo", ("ubuf", u)], [("ubuf", u)])
                    vcopy(uhalo[:, idx, :], ubuf[:, u, 512:514], [("ubuf", u)], ["uhalo"])
                    for k in range(0, 2):
                        stt(tmp[:, u, :], ubuf[:, u, k:k + 512], fw(k), tmp[:, u, :], ALU.mult, ALU.add,
                            [("ubuf", u), ("vec", l), ("tmp", u)], [("tmp", u)])
                act(ubuf[:, 0, 0:512], tmp[:, 0, :], AF.Tanh, [("tmp", 0), ("ubuf", 0)], [("ubuf", 0)], scale=0.5)
                stt(tmp[:, 0, :], ubuf[:, 0, 0:512], 1.0, tmp[:, 0, :], ALU.add, ALU.mult, [("ubuf", 0), ("tmp", 0)], [("tmp", 0)])
                ptt(gT[:, gi, :], tmp[:, 0, :], tmp[:, 1, :], ALU.mult, [("tmp", 0), ("tmp", 1)], [("gT", gi)])
            for c in range(8):
                bank = 6 + c % 2
                for kh in range(2):
                    ring, rk = wload("F", l, ("DOWN", 2 * c + kh))
                    w3 = ring[:, 0:1408].rearrange("p (k n) -> p k n", k=11)
                    for k in range(11):
                        kk = kh * 11 + k
                        mm(pb[bank][:], w3[:, k, :], gT[:, kk, :], kk == 0, kk == 21, [rk, ("gT", kk)], [("ps", bank)])
                stt(xT[:, c, :], pb[bank][:], hgf[:, l, c:c + 1], xT[:, c, :], ALU.mult, ALU.add,
                    [("ps", bank), ("gs", l), ("xT", c)], [("xT", c)])
            if l == 0 and j == 0:
                dbg_dump("xout", xT[:], xTK)
            if l < L - 1:
                dma(x1T[:, :, t0:t0 + T].rearrange("c p t -> p c t"), xT[:], xTK, [("x1T", j)], q="gpsimd")
            else:
                for c in range(8):
                    sq_acc(xT[:, c, :], ("xT", c), 6, c == 0, c == 7, "F")
                act(pb[7][:], pb[6][:], AF.Ln, [("ps", 6)], [("ps", 7)], bias=EPS, scale=1.0 / D)
                act(pb[7][:], pb[7][:], AF.Exp, [("ps", 7)], [("ps", 7)], scale=-0.5)
                for c in range(8):
                    stt(xT[:, c, :], xT[:, c, :], vec[:, l, VC["fin"] + c:VC["fin"] + c + 1], pb[7][:], ALU.mult, ALU.mult,
                        [("xT", c), ("vec", l), ("ps", 7)], [("xT", c)])
                nbf = 0
                for blk in range(4):
                    hs = blk % 2
                    otok = Hf[:, 4 * hs:4 * hs + 4, :].rearrange("p a b -> p (a b)").bitcast(F32)
                    hkeys = [("Hf", 4 * hs + i) for i in range(4)]
                    for half in range(2):
                        bank = 6 + nbf % 2
                        nbf += 1
                        for q in range(4):
                            c = half * 4 + q
                            tr(pb[bank][:, q * 128:(q + 1) * 128], xT[:, c, blk * 128:(blk + 1) * 128], ident,
                               [("xT", c), "consts"], [("ps", bank)])
                        vcopy(otok[:, half * 512:(half + 1) * 512], pb[bank][:], [("ps", bank)], hkeys)
                    dma(out[t0 + blk * 128:t0 + (blk + 1) * 128, :], otok, hkeys, [("out", t0 + blk * 128)], q="gpsimd")

        def record(fn, *a):
            r = []
            _radd = P.add
            P.add = lambda *a_, **k_: r.append((a_, k_))
            fn(*a)
            P.add = _radd
            return r

        tiles = [(l, j) for l in range(n_layers) for j in range(n_tiles)]
        P.bg = [(lambda sp=sp: conv_subpiece(sp, "gpsimd", "gpsimd")) for sp in range(n_early, NSPC)]
        bg_sp.extend(range(n_early, NSPC))
        bg_sp.append(10 ** 9)
        P.bg.append(lambda: vmemset(V[:, 16:24, :, 64:65], 1.0, [("V", q) for q in range(16, 24)]))
        P.bg_every = int(os.environ.get("KBG", "48"))
        BUR = int(os.environ.get("KBUR", "2"))

        P.bg_sp = bg_sp

        def flush_bg():
            while P.bg:
                if bg_sp:
                    bg_sp.pop(0)
                P._in_bg = True
                P.bg.pop(0)()
                P._in_bg = False

        def tail_and_F(l, j):
            emit_tail(l, j)
            emit_F(l, j)

        emit_M(*tiles[0])
        if not TAILF:
            emit_tail(*tiles[0])
        for g, tl in enumerate(tiles):
            if g + 1 < len(tiles) and merge:
                if tiles[g + 1] == (0, 4):
                    flush_bg()
                rF = record(emit_F, *tl)
                rM = record(emit_M, *tiles[g + 1])
                rM = [x_ for x_ in rM if x_[0][0] != "MARK"]

                def merge2(rA, rB):
                    nA, nB = len(rA), len(rB)
                    iA = iB = 0
                    while iA < nA or iB < nB:
                        fA = iA / nA if nA else 1.0
                        fB = iB / nB if nB else 1.0
                        if iB < nB and (fB <= fA or iA >= nA):
                            for (a_, k_) in rB[iB:iB + BUR]:
                                P.add(*a_, **k_)
                            iB += BUR
                        else:
                            for (a_, k_) in rA[iA:iA + BUR]:
                                P.add(*a_, **k_)
                            iA += BUR

                if TAILF:
                    rT = record(emit_tail, *tl)
                    cut = next(i_ for i_, (a_, k_) in enumerate(rM)
                               if any(isinstance(w_, tuple) and w_[0] == "Hm" for w_ in k_.get("writes", ())))
                    merge2(rT, rM[:cut])
                    rM = rM[cut:]
                merge2(rF, rM)
                if not TAILF:
                    emit_tail(*tiles[g + 1])
            else:
                if TAILF:
                    emit_tail(*tl)
                emit_F(*tl)
                if g + 1 < len(tiles):
                    emit_M(*tiles[g + 1])
                    if not TAILF:
                        emit_tail(*tiles[g + 1])
        while P.bg:
            P._in_bg = True
            P.bg.pop(0)()
            P._in_bg = False
        if os.environ.get("KDEBUG"):
            print("P.ops:", len(P.ops))
        P.emit()
    return nc


def _consts():
    c = np.zeros((128, NCONST), np.float32)
    c[:, CC["ident"]:CC["ident"] + 128] = np.eye(128, dtype=np.float32)
    j = np.arange(128)
    c[:, CC["tri"]:CC["tri"] + 128] = (j[:, None] <= j[None, :]).astype(np.float32)
    nm = np.where(j[:, None] > j[None, :], -30000.0, 0.0).astype(np.float32)
    c[:, CC["negm"]:CC["negm"] + 128] = nm
    c[:, CC["sgn"]:CC["sgn"] + 8] = -1.0
    c[:, CC["sgn"] + 8:CC["sgn"] + 16] = 1.0
    c[:, CC["ones"]:CC["ones"] + 128] = 1.0
    return c


def _kmaj(w, nk):
    n = w.shape[1]
    return np.ascontiguousarray(w.reshape(nk, 128, n).transpose(1, 0, 2)).reshape(128, nk * n)


def _chunkvec(v):
    return np.ascontiguousarray(v.reshape(-1, 128).T)


def _prep(inp):
    wall = np.zeros((128, WPAD), np.float32)
    modw = np.zeros((L * 12, 128, 4096), np.float32)
    vec = np.zeros((L, 128, NV), np.float32)
    bv = np.zeros((L, 40), np.float32)
    for l in range(L):
        w_in = inp["w_in"][l]
        q, k, v = w_in[:, 0:512], w_in[:, 512:1024], w_in[:, 1024:1536]
        f = w_in[:, 1536:1544]
        z = w_in[:, 1544:2056]
        xbc = w_in[:, 2056:3080]
        dt = w_in[:, 3080:3088]
        fm = np.concatenate([q, k, z, xbc], axis=1)
        base = l * WL
        def put(key, arr):
            off, n = WB[key]
            assert arr.shape == (128, n), (key, arr.shape, n)
            wall[:, base + off:base + off + n] = arr
        for c in range(10):
            put(("INF", c), _kmaj(fm[:, c * 256:(c + 1) * 256], 8))
        for c in range(2):
            put(("INV", c), _kmaj(v[:, c * 256:(c + 1) * 256], 8))
        put(("INS",), _kmaj(np.concatenate([f, dt], axis=1), 8))
        wo = inp["w_out"][l]
        for c in range(4):
            put(("OUT", c), _kmaj(wo[:, c * 256:(c + 1) * 256], 8))
        up = inp["ffn_w_up"][l]
        for b in range(22):
            blk = np.concatenate([up[:, b * 128:(b + 1) * 128], up[:, DFF + b * 128:DFF + (b + 1) * 128]], axis=1)
            put(("UP", b), _kmaj(blk, 8))
        dn = inp["ffn_w_down"][l]
        for m in range(16):
            c_, kh = m // 2, m % 2
            put(("DOWN", m), _kmaj(dn[kh * 1408:(kh + 1) * 1408, c_ * 128:(c_ + 1) * 128], 11))
        mw = inp["mod_w"][l]
        for b in range(12):
            modw[l * 12 + b] = _kmaj(mw[:, b * 512:(b + 1) * 512], 8)
        vec[l, :, VC["modb"]:VC["modb"] + 48] = _chunkvec(inp["mod_b"][l])
        vec[l, :, VC["nmg"]:VC["nmg"] + 8] = _chunkvec(inp["norm_mix_g"][l])
        vec[l, :, VC["nfg"]:VC["nfg"] + 8] = _chunkvec(inp["norm_ffn_g"][l])
        vec[l, :, VC["ang"]:VC["ang"] + 4] = _chunkvec(inp["attn_norm_g"][l])
        vec[l, :, VC["sng"]:VC["sng"] + 4] = _chunkvec(inp["ssd_norm_g"][l])
        for kk in range(4):
            vec[l, :, VC["cw"] + kk * 8:VC["cw"] + kk * 8 + 8] = _chunkvec(inp["ssd_conv_w"][l][kk])
        vec[l, :, VC["cb"]:VC["cb"] + 8] = _chunkvec(inp["ssd_conv_b"][l])
        for kk in range(3):
            vec[l, :, VC["fcw"] + kk * 44:VC["fcw"] + kk * 44 + 44] = _chunkvec(inp["ffn_conv_w"][l][kk])
        vec[l, :, VC["fcb"]:VC["fcb"] + 44] = _chunkvec(inp["ffn_conv_b"][l])
        vec[l, :, VC["fin"]:VC["fin"] + 8] = _chunkvec(inp["final_g"])
        bv[l, 0:8] = inp["fox_forget_b"][l]
        bv[l, 8:16] = inp["ssd_dt_bias"][l]
        bv[l, 16:24] = inp["ssd_a_log"][l]
        bv[l, 24:32] = inp["ssd_d"][l]
    return wall, modw, vec, bv


def kernel(**inputs):
    inp = {k: np.asarray(v, dtype=np.float32) for k, v in inputs.items()}
    wall, modw, vec, bv = _prep(inp)
    consts = _consts()
    nc = build_nc()
    x = inp["x"]
    c = inp["c"]
    in_maps = []
    for b in range(8):
        in_maps.append({
            "x": np.ascontiguousarray(x[b]),
            "c": np.ascontiguousarray(c[b].reshape(8, 128).T),
            "wall": wall, "modw": modw, "vec": vec, "bv": bv, "consts": consts,
            "grow": np.ascontiguousarray(inp["attn_norm_g"]),
        })
    res = run_bass_kernel_spmd(nc, in_maps, core_ids=list(range(8)))
    return np.stack([np.asarray(r["out"], dtype=np.float32) for r in res.results], axis=0)
```
